# Optimizing a Trainium2 kernel written in Bass

```python
import math
import jax, jax.numpy as jnp
from jax import lax
import numpy as np

D_MODEL = 2048
BATCH = 2
SEQ = 8192
DEPTH = 2

CHUNK = 64
N_META = 16
N_A_LAYERS = DEPTH // 2
N_B_LAYERS = DEPTH - N_A_LAYERS
SSD_EXPAND = 2
SSD_D_INNER = SSD_EXPAND * D_MODEL
SSD_HEADDIM = 64
SSD_HEADS = SSD_D_INNER // SSD_HEADDIM
SSD_GROUPS = 8
SSD_HEADS_PER_GROUP = SSD_HEADS // SSD_GROUPS
SSD_STATE = 128
SSD_CONV = 4
SSD_CONV_DIM = SSD_D_INNER + 2 * SSD_GROUPS * SSD_STATE
SSD_IN_DIM = SSD_D_INNER + SSD_CONV_DIM + SSD_HEADS
SB_HEAD_DIM = 128
SB_HEADS = D_MODEL // SB_HEAD_DIM
SB_WIDTH = SB_HEADS * SB_HEAD_DIM
Q_BLOCK = 128
FFN_HIDDEN = -(-8 * D_MODEL // (3 * 256)) * 256
EPS = 1e-6

kernel_name = "yoco_ssd_stickbreaking_hybrid"


def rmsnorm(x, g):
    xf = x.astype(jnp.float32)
    y = xf * lax.rsqrt(jnp.mean(xf * xf, axis=-1, keepdims=True) + EPS)
    return (y * g.astype(jnp.float32)).astype(x.dtype)


def causal_depthwise_conv(x, w, b):
    k, c = w.shape
    out = lax.conv_general_dilated(
        x, w[:, None, :].astype(x.dtype), window_strides=(1,), padding=((k - 1, 0),),
        dimension_numbers=("NWC", "WIO", "NWC"), feature_group_count=c)
    return out + b.astype(x.dtype)


def ssd_chunked_scan(xs, dt, A, Bm, Cm):
    b, t = xs.shape[:2]
    nc = t // CHUNK

    def to_chunks(a):
        return a.reshape((b, nc, CHUNK) + a.shape[2:]).swapaxes(0, 1)

    idx = jnp.arange(CHUNK)
    causal = (idx[:, None] >= idx[None, :])[None, :, :, None, None]

    def step(state, inp):
        xc, dtc, bc, cc = inp
        cum = jnp.cumsum(dtc * A, axis=1)
        seg = cum[:, :, None] - cum[:, None, :]
        decay = jnp.exp(jnp.where(causal, seg, -jnp.inf))
        cb = jnp.einsum("bqgn,bsgn->bqsg", cc, bc)
        xdt = xc * dtc[..., None]
        y_diag = jnp.einsum("bqsg,bqsgr,bsgrp->bqgrp", cb, decay, xdt)
        y_off = jnp.einsum("bqgn,bgrpn->bqgrp", cc, state) * jnp.exp(cum)[..., None]
        last = cum[:, -1]
        w = jnp.exp(last[:, None] - cum) * dtc
        new_state = state * jnp.exp(last)[..., None, None] + jnp.einsum(
            "bsgn,bsgr,bsgrp->bgrpn", bc, w, xc)
        return new_state, y_diag + y_off

    state0 = jnp.zeros((b, SSD_GROUPS, SSD_HEADS_PER_GROUP, SSD_HEADDIM, SSD_STATE), jnp.float32)
    _, ys = lax.scan(step, state0, (to_chunks(xs), to_chunks(dt), to_chunks(Bm), to_chunks(Cm)))
    return ys.swapaxes(0, 1).reshape(xs.shape)


def ssd_mixer(u, in_proj, conv_w, conv_b, dt_bias, a_log, d_skip, norm_g, out_proj):
    b, l, _ = u.shape
    zxbcdt = u @ in_proj
    z = zxbcdt[..., :SSD_D_INNER]
    xbc = zxbcdt[..., SSD_D_INNER:SSD_D_INNER + SSD_CONV_DIM]
    dt = zxbcdt[..., SSD_D_INNER + SSD_CONV_DIM:]
    xbc = jax.nn.silu(causal_depthwise_conv(xbc, conv_w, conv_b)).astype(jnp.float32)
    dt = jax.nn.softplus(dt.astype(jnp.float32) + dt_bias.astype(jnp.float32))
    pad = CHUNK - N_META
    xbc = jnp.pad(xbc, ((0, 0), (pad, 0), (0, 0)))
    dt = jnp.pad(dt, ((0, 0), (pad, 0), (0, 0)))
    t = l + pad
    gn = SSD_GROUPS * SSD_STATE
    xs = xbc[..., :SSD_D_INNER].reshape(b, t, SSD_GROUPS, SSD_HEADS_PER_GROUP, SSD_HEADDIM)
    bm = xbc[..., SSD_D_INNER:SSD_D_INNER + gn].reshape(b, t, SSD_GROUPS, SSD_STATE)
    cm = xbc[..., SSD_D_INNER + gn:].reshape(b, t, SSD_GROUPS, SSD_STATE)
    dt = dt.reshape(b, t, SSD_GROUPS, SSD_HEADS_PER_GROUP)
    a = -jnp.exp(a_log.astype(jnp.float32)).reshape(SSD_GROUPS, SSD_HEADS_PER_GROUP)
    y = ssd_chunked_scan(xs, dt, a, bm, cm)
    y = y + xs * d_skip.astype(jnp.float32).reshape(SSD_GROUPS, SSD_HEADS_PER_GROUP)[..., None]
    y = y[:, pad:].reshape(b, l, SSD_D_INNER)
    g = (y * jax.nn.silu(z.astype(jnp.float32))).reshape(b, l, SSD_GROUPS, SSD_D_INNER // SSD_GROUPS)
    g = g * lax.rsqrt(jnp.mean(g * g, axis=-1, keepdims=True) + EPS)
    g = g.reshape(b, l, SSD_D_INNER) * norm_g.astype(jnp.float32)
    return g.astype(u.dtype) @ out_proj


def stick_breaking_attention(u, w_q, k, v, w_o):
    b, l, _ = u.shape
    q = (u @ w_q).reshape(b, l, SB_HEADS, SB_HEAD_DIM)
    nb = -(-l // Q_BLOCK)
    lp = nb * Q_BLOCK
    q = jnp.pad(q, ((0, 0), (0, lp - l), (0, 0), (0, 0)))
    qb = q.reshape(b, nb, Q_BLOCK, SB_HEADS, SB_HEAD_DIM).swapaxes(0, 1)
    starts = jnp.arange(nb) * Q_BLOCK
    kpos = jnp.arange(l)
    scale = SB_HEAD_DIM ** -0.5

    def block(args):
        qblk, start = args
        qpos = start + jnp.arange(Q_BLOCK)
        z = jnp.einsum("bqhd,bkhd->bhqk", qblk, k).astype(jnp.float32) * scale
        before = (kpos[None, :] < qpos[:, None])[None, None]
        log_keep = jnp.where(before, jax.nn.log_sigmoid(-z), 0.0)
        tail = lax.cumsum(log_keep, axis=3, reverse=True) - log_keep
        att = jnp.where(before, jnp.exp(jax.nn.log_sigmoid(z) + tail), 0.0)
        return jnp.einsum("bhqk,bkhd->bqhd", att.astype(v.dtype), v)

    o = lax.map(block, (qb, starts))
    o = o.swapaxes(0, 1).reshape(b, lp, SB_WIDTH)[:, :l]
    return o @ w_o


def swiglu(u, w_gate_up, w_down):
    gu = u @ w_gate_up
    return (jax.nn.silu(gu[..., :FFN_HIDDEN]) * gu[..., FFN_HIDDEN:]) @ w_down


def setup_inputs(seed: int = 0) -> dict:
    key = jax.random.key(seed)
    ks = jax.random.split(key, 24)
    f32 = jnp.float32

    def nrm(k, shape, scale):
        return jax.random.normal(k, shape, f32) * scale

    x = nrm(ks[0], (BATCH, SEQ, D_MODEL), 1.0)
    meta_tokens = nrm(ks[1], (N_META, D_MODEL), 1.0)
    norm_mix = 1.0 + nrm(ks[2], (DEPTH, D_MODEL), 0.02)
    norm_ffn = 1.0 + nrm(ks[3], (DEPTH, D_MODEL), 0.02)
    ssd_in_proj = nrm(ks[4], (N_A_LAYERS, D_MODEL, SSD_IN_DIM), D_MODEL ** -0.5)
    ssd_conv_w = nrm(ks[5], (N_A_LAYERS, SSD_CONV, SSD_CONV_DIM), SSD_CONV ** -0.5)
    ssd_conv_b = nrm(ks[6], (N_A_LAYERS, SSD_CONV_DIM), 0.02)
    dt0 = jnp.exp(jax.random.uniform(ks[7], (N_A_LAYERS, SSD_HEADS), f32,
                                     math.log(1e-3), math.log(1e-1)))
    ssd_dt_bias = dt0 + jnp.log(-jnp.expm1(-dt0))
    ssd_a_log = jnp.log(jax.random.uniform(ks[8], (N_A_LAYERS, SSD_HEADS), f32, 1.0, 16.0))
    ssd_d = 1.0 + nrm(ks[9], (N_A_LAYERS, SSD_HEADS), 0.02)
    ssd_norm = 1.0 + nrm(ks[10], (N_A_LAYERS, SSD_D_INNER), 0.02)
    ssd_out_proj = nrm(ks[11], (N_A_LAYERS, SSD_D_INNER, D_MODEL), SSD_D_INNER ** -0.5)
    kv_norm = 1.0 + nrm(ks[12], (D_MODEL,), 0.02)
    w_kv = nrm(ks[13], (D_MODEL, 2 * SB_WIDTH), D_MODEL ** -0.5)
    sb_w_q = nrm(ks[14], (N_B_LAYERS, D_MODEL, SB_WIDTH), D_MODEL ** -0.5)
    sb_w_o = nrm(ks[15], (N_B_LAYERS, SB_WIDTH, D_MODEL), SB_WIDTH ** -0.5)
    ffn_gate_up = nrm(ks[16], (DEPTH, D_MODEL, 2 * FFN_HIDDEN), D_MODEL ** -0.5)
    ffn_down = nrm(ks[17], (DEPTH, FFN_HIDDEN, D_MODEL), FFN_HIDDEN ** -0.5)
    final_norm = 1.0 + nrm(ks[18], (D_MODEL,), 0.02)
    return {"x": x, "meta_tokens": meta_tokens, "norm_mix": norm_mix, "norm_ffn": norm_ffn,
            "ssd_in_proj": ssd_in_proj, "ssd_conv_w": ssd_conv_w, "ssd_conv_b": ssd_conv_b,
            "ssd_dt_bias": ssd_dt_bias, "ssd_a_log": ssd_a_log, "ssd_d": ssd_d,
            "ssd_norm": ssd_norm, "ssd_out_proj": ssd_out_proj, "kv_norm": kv_norm,
            "w_kv": w_kv, "sb_w_q": sb_w_q, "sb_w_o": sb_w_o, "ffn_gate_up": ffn_gate_up,
            "ffn_down": ffn_down, "final_norm": final_norm}


def reference(x, meta_tokens, norm_mix, norm_ffn, ssd_in_proj, ssd_conv_w, ssd_conv_b,
              ssd_dt_bias, ssd_a_log, ssd_d, ssd_norm, ssd_out_proj, kv_norm, w_kv,
              sb_w_q, sb_w_o, ffn_gate_up, ffn_down, final_norm):
    b = x.shape[0]
    meta = jnp.broadcast_to(meta_tokens[None].astype(x.dtype), (b, N_META, D_MODEL))
    h = jnp.concatenate([meta, x], axis=1)
    l = h.shape[1]
    k_shared = None
    v_shared = None
    for layer in range(DEPTH):
        u = rmsnorm(h, norm_mix[layer])
        if layer < N_A_LAYERS:
            i = layer
            h = h + ssd_mixer(u, ssd_in_proj[i], ssd_conv_w[i], ssd_conv_b[i], ssd_dt_bias[i],
                              ssd_a_log[i], ssd_d[i], ssd_norm[i], ssd_out_proj[i])
        else:
            j = layer - N_A_LAYERS
            h = h + stick_breaking_attention(u, sb_w_q[j], k_shared, v_shared, sb_w_o[j])
        h = h + swiglu(rmsnorm(h, norm_ffn[layer]), ffn_gate_up[layer], ffn_down[layer])
        if layer == N_A_LAYERS - 1:
            kv = rmsnorm(h, kv_norm) @ w_kv
            k_shared = kv[..., :SB_WIDTH].reshape(b, l, SB_HEADS, SB_HEAD_DIM)
            v_shared = kv[..., SB_WIDTH:].reshape(b, l, SB_HEADS, SB_HEAD_DIM)
    return rmsnorm(h, final_norm)[:, N_META:]
```

```python
import contextlib
import numpy as np
import ml_dtypes
import concourse.bass as bass
import concourse.mybir as mybir
from concourse.bass_utils import run_bass_kernel_spmd

F32 = mybir.dt.float32
BF16 = mybir.dt.bfloat16
AF = mybir.ActivationFunctionType
ALU = mybir.AluOpType

NCORES = 8
D = 2048
KD = D // 128
NMETA = 16
DI = 4096
GW = 1288
FF = 5632
KF = FF // 128
EPS = 1e-6


class Res:
    __slots__ = ("name", "w", "r")

    def __init__(self, name=""):
        self.name = name
        self.w = None
        self.r = []


class K:
    def __init__(self, nc, n_dma_sems=12):
        self.nc = nc
        self.engs = {"pe": nc.tensor, "dve": nc.vector, "act": nc.scalar,
                     "pool": nc.gpsimd, "sp": nc.sync}
        self.sems = {}
        self.cnt = {}
        self._ctx = contextlib.ExitStack()
        for n in self.engs:
            self.sems[n] = self._ctx.enter_context(nc.semaphore("s_" + n))
            self.cnt[n] = 0
        self.dma_pool = {}
        for q in ("sp", "pool", "act"):
            lst = []
            for i in range(n_dma_sems):
                key = "d_%s%d" % (q, i)
                self.sems[key] = self._ctx.enter_context(nc.semaphore(key))
                self.cnt[key] = 0
                lst.append(key)
            self.dma_pool[q] = lst
        self.dma_rr = {q: 0 for q in self.dma_pool}
        self.known = {n: {} for n in self.engs}
        self.n_wait = 0
        self.n_ins = 0
        self.limit = None

    def _wait(self, eng, tok):
        if tok is None:
            return
        key, val = tok
        kn = self.known[eng]
        if kn.get(key, 0) >= val:
            return
        self.engs[eng].wait_ge(self.sems[key], val)
        kn[key] = val
        self.n_wait += 1

    def _deps(self, eng, reads, writes, same):
        for r in reads:
            if r.w is not None and not (r.w[0] == eng and not same):
                self._wait(eng, r.w)
        for w in writes:
            if w.w is not None and w.w[0] != eng:
                self._wait(eng, w.w)
            for t in w.r:
                if t[0] != eng:
                    self._wait(eng, t)

    def _mark(self, tok, reads, writes):
        for w in writes:
            w.w = tok
            w.r = []
        for r in reads:
            if r in writes:
                continue
            r.r.append(tok)
            if len(r.r) > 16:
                best = {}
                for k_, v in r.r:
                    if best.get(k_, 0) < v:
                        best[k_] = v
                r.r = list(best.items())

    def op(self, eng, fn, reads=(), writes=(), same=None):
        if same is None:
            same = (eng != "pe")
        if self.limit is not None and self.n_ins >= self.limit:
            return None
        self._deps(eng, reads, writes, same)
        ins = fn(self.engs[eng])
        self.cnt[eng] += 1
        ins.then_inc(self.sems[eng], 1)
        tok = (eng, self.cnt[eng])
        self._mark(tok, reads, writes)
        self.n_ins += 1
        return tok

    def dma(self, q, out, in_, reads=(), writes=(), **kw):
        pool = self.dma_pool[q]
        key = pool[self.dma_rr[q] % len(pool)]
        self.dma_rr[q] += 1
        if self.cnt[key] > 0:
            self._wait(q, (key, self.cnt[key]))
        self._deps(q, reads, writes, True)
        ins = self.engs[q].dma_start(out=out, in_=in_, **kw)
        self.cnt[key] += 16
        ins.then_inc(self.sems[key], 16)
        tok = (key, self.cnt[key])
        self._mark(tok, reads, writes)
        self.n_ins += 1
        return tok

    def collective(self, kind, ins_ap, outs_ap, groups, reads=(), writes=()):
        eng = "pool"
        self._deps(eng, reads, writes, True)
        ins = self.nc.gpsimd.collective_compute(
            kind, ALU.bypass, replica_groups=groups, ins=[ins_ap], outs=[outs_ap])
        self.cnt[eng] += 1
        ins.then_inc(self.sems[eng], 1)
        tok = (eng, self.cnt[eng])
        self._mark(tok, reads, writes)
        return tok

    def barrier(self):
        toks = [(n, c) for n, c in self.cnt.items() if c > 0]
        for e in self.engs:
            for t in toks:
                if t[0] == e:
                    continue
                self._wait(e, t)

    def final_wait(self, resources):
        for e in self.engs:
            for r in resources:
                self._wait(e, r.w)


class Tl:
    def __init__(self, stack, nc, name, shape, dtype, psum=False, nres=1):
        f = nc.psum_tensor if psum else nc.sbuf_tensor
        self.t = stack.enter_context(f(name, list(shape), dtype))
        self.res = [Res(name + str(i)) for i in range(nres)]
        self.r = self.res[0]

    def __getitem__(self, idx):
        return self.t[idx]


def bc(ap, shape):
    return ap.broadcast_to(list(shape))


class MK:
    def __init__(self, S, dumps=(), phases=(0, 1, 2, 3, 4), fused=True, dbg=None):
        self.dbg = dbg or {}
        self.phases = list(phases)
        self.fused = fused
        self.ext_in = []
        self.S = S
        self.TPC = S // 4
        self.NT = self.TPC // 512
        self.dumps = list(dumps)
        nc = bass.Bass("TRN2", target_bir_lowering=False)
        self.nc = nc
        self.k = K(nc)
        self.k.limit = self.dbg.get('limit')
        self.dram = {}
        self.dres = {}
        self.ext_out = []

    def din(self, name, shape, dtype=F32):
        if name in self.dram:
            return self.dram[name]
        t = self.nc.dram_tensor(name, list(shape), dtype, kind="ExternalInput")
        self.dram[name] = t
        self.dres[name] = Res(name)
        self.ext_in.append(name)
        return t

    def buf(self, name, shape, dtype, producer, cur):
        if name in self.dram:
            return self.dram[name]
        if self.fused:
            return self.dint(name, shape, dtype)
        if producer == cur:
            return self.dout(name, shape, dtype)
        return self.din(name, shape, dtype)

    def dint(self, name, shape, dtype):
        t = self.nc.dram_tensor(name, list(shape), dtype)
        self.dram[name] = t
        self.dres[name] = Res(name)
        return t

    def dout(self, name, shape, dtype=F32):
        t = self.nc.dram_tensor(name, list(shape), dtype, kind="ExternalOutput")
        self.dram[name] = t
        self.dres[name] = Res(name)
        self.ext_out.append(name)
        return t

    def tile(self, st, name, shape, dtype, psum=False, nres=1):
        return Tl(st, self.nc, name, shape, dtype, psum, nres)

    def load_consts(self, st):
        k = self.k
        self.din("ident_bf", [128, 128], BF16)
        self.din("ident_f32", [128, 128])
        self.din("gains", [128, 6, KD])
        self.c_ident_bf = self.tile(st, "c_ident_bf", [128, 128], BF16)
        self.c_ident_f = self.tile(st, "c_ident_f", [128, 128], F32)
        self.c_gains = self.tile(st, "c_gains", [128, 6, KD], F32)
        k.dma("sp", self.c_ident_bf[:], self.dram["ident_bf"].ap(), writes=[self.c_ident_bf.r])
        k.dma("sp", self.c_ident_f[:], self.dram["ident_f32"].ap(), writes=[self.c_ident_f.r])
        k.dma("sp", self.c_gains[:], self.dram["gains"].ap(), writes=[self.c_gains.r])

    def phase0(self):
        k, nc, TPC = self.k, self.nc, self.TPC
        with contextlib.ExitStack() as st:
            xt = [self.tile(st, "p0_x%d" % i, [128, D], F32) for i in range(2)]
            junk = self.tile(st, "p0_junk", [128, D], BF16)
            un = [self.tile(st, "p0_un%d" % i, [128, D], BF16) for i in range(2)]
            ss = [self.tile(st, "p0_ss%d" % i, [128, 2], F32) for i in range(2)]
            pT = [self.tile(st, "p0_pT%d" % i, [128, KD, 128], BF16, psum=True) for i in range(2)]
            uT = [self.tile(st, "p0_uT%d" % i, [128, KD, 512], BF16) for i in range(2)]
            x_loc = self.din("x_loc", [TPC, D]).ap()
            meta = self.din("meta", [NMETA, D]).ap()
            uT_loc = self.buf("uT_loc", [D, TPC], BF16, 0, 0).ap().rearrange("(c p) t -> p c t", p=128)
            uT_meta = self.buf("uT_meta", [D, NMETA], BF16, 0, 0).ap().rearrange("(c p) t -> p c t", p=128)
            g0 = self.c_gains[:, 0, :]
            ntile = TPC // 128
            tiles = [(i * 128, 128, False) for i in range(ntile)] + [(0, NMETA, True)]
            for ti, (r0, n, is_meta) in enumerate(tiles):
                b = ti % 2
                src = meta[0:n, :] if is_meta else x_loc[r0:r0 + n, :]
                k.dma("sp", xt[b][0:n, :], src, writes=[xt[b].r])
                k.op("act", lambda e: e.activation(out=junk[0:n, :], in_=xt[b][0:n, :], func=AF.Square,
                                                   accum_out=ss[b][0:n, 0:1]),
                     reads=[xt[b].r], writes=[junk.r, ss[b].r])
                k.op("act", lambda e: e.activation(out=ss[b][0:n, 1:2], in_=ss[b][0:n, 0:1], func=AF.Ln,
                                                   scale=1.0 / D, bias=self.c_eps[0:n, 0:1]),
                     reads=[ss[b].r], writes=[ss[b].r])
                k.op("act", lambda e: e.activation(out=ss[b][0:n, 1:2], in_=ss[b][0:n, 1:2], func=AF.Exp,
                                                   scale=-0.5),
                     reads=[ss[b].r], writes=[ss[b].r])
                k.op("dve", lambda e: e.tensor_scalar(out=un[b][0:n, :], in0=xt[b][0:n, :],
                                                      scalar1=ss[b][0:n, 1:2], scalar2=None, op0=ALU.mult),
                     reads=[xt[b].r, ss[b].r], writes=[un[b].r])
                for c in range(KD):
                    k.op("pe", lambda e: e.transpose(out=pT[b][:, c, 0:n], in_=un[b][0:n, c * 128:(c + 1) * 128],
                                                     identity=self.c_ident_bf[0:n, 0:n]),
                         reads=[un[b].r, self.c_ident_bf.r], writes=[pT[b].r])
                if is_meta:
                    ub, col = (ti // 4 + 1) % 2, 0
                else:
                    ub, col = (ti // 4) % 2, (ti % 4) * 128
                k.op("dve", lambda e: e.tensor_tensor(out=uT[ub][:, :, col:col + n], in0=pT[b][:, :, 0:n],
                                                      in1=bc(g0.unsqueeze(2), [128, KD, n]), op=ALU.mult),
                     reads=[pT[b].r, self.c_gains.r], writes=[uT[ub].r])
                if is_meta:
                    k.dma("sp", uT_meta, uT[ub][:, :, 0:n], reads=[uT[ub].r], writes=[self.dres["uT_meta"]])
                elif ti % 4 == 3:
                    t0 = (ti // 4) * 512
                    k.dma("sp", uT_loc[:, :, t0:t0 + 512], uT[ub][:, :, :], reads=[uT[ub].r],
                          writes=[self.dres["uT_loc"]])
            k.barrier()
        if self.fused:
            self.dint("uT_all", [NCORES * D, TPC], BF16)
            k.collective("AllGather", self.dram["uT_loc"].ap().opt(), self.dram["uT_all"].ap().opt(),
                         [list(range(NCORES))], reads=[self.dres["uT_loc"]], writes=[self.dres["uT_all"]])


    def phase1(self):
        k, nc, S, TPC = self.k, self.nc, self.S, self.TPC
        self.din("w_in_g", [D, GW])
        self.din("conv_w", [128, 6, 4])
        self.din("conv_b", [128, 6])
        self.din("dt_bias", [1, 8])
        self.din("ones_bf", [128, 128], BF16)
        self.din("alog", [1, 8])
        self.din("dvec", [1, 8])
        self.din("ssd_norm", [1, 512])
        self.din("tri_bf", [128, 128], BF16)
        self.din("sl_bf", [128, 128], BF16)
        self.din("tri_f32", [128, 128])
        self.din("ones_f32", [128, 128])
        self.buf("uT_all", [NCORES * D, TPC], BF16, 0, 1)
        self.buf("uT_meta", [D, NMETA], BF16, 0, 1)
        self.buf("g_send", [NCORES, 512, TPC + NMETA], BF16, 1, 1)
        with contextlib.ExitStack() as st:
            T_ = lambda n, s, d, **kw: self.tile(st, "p1_" + n, s, d, **kw)
            Wg = T_("Wg", [128, KD, GW], BF16)
            k.dma("pool", Wg[:], self.dram["w_in_g"].ap().rearrange("(c p) n -> p c n", p=128), writes=[Wg.r])
            if self.fused and self.dbg.get("hoist_w", 1):
                self.phase5()
            cw = T_("cw", [128, 6, 4], F32); cb = T_("cb", [128, 6], F32)
            dtb = T_("dtb", [128, 1, 8], F32)
            ones_bf = T_("ones_bf", [128, 128], BF16)
            Abc = T_("Abc", [128, 1, 8], F32); Dbc = T_("Dbc", [128, 1, 8], F32)
            ngb = T_("ngb", [128, 1, 512], F32)
            tri_bf = T_("tri_bf", [128, 128], BF16); sl_bf = T_("sl_bf", [128, 128], BF16)
            tri_f = T_("tri_f", [128, 128], F32); ones_f = T_("ones_f", [128, 128], F32)
            for tl, nm in ((cw, "conv_w"), (cb, "conv_b"), (ones_bf, "ones_bf"), (tri_bf, "tri_bf"), (sl_bf, "sl_bf"),
                           (tri_f, "tri_f32"), (ones_f, "ones_f32")):
                k.dma("sp", tl[:], self.dram[nm].ap(), writes=[tl.r])
            for tl, nm in ((Abc, "alog"), (Dbc, "dvec"), (ngb, "ssd_norm"), (dtb, "dt_bias")):
                k.dma("sp", tl[:], self.dram[nm].ap().partition_broadcast(128), writes=[tl.r])
            k.op("act", lambda e: e.activation(out=Abc[:], in_=Abc[:], func=AF.Exp), reads=[Abc.r], writes=[Abc.r])
            k.op("dve", lambda e: e.tensor_scalar(out=Abc[:], in0=Abc[:], scalar1=-1.0, scalar2=None, op0=ALU.mult),
                 reads=[Abc.r], writes=[Abc.r])
            Wdt = T_("Wdt", [128, KD, 16], BF16)
            k.op("dve", lambda e: e.tensor_copy(out=Wdt[:], in_=Wg[:, :, 1272:1288]), reads=[Wg.r], writes=[Wdt.r])
            A2 = Abc[:, 0, :]
            D2 = Dbc[:, 0, :]
            ng2 = ngb[:, 0, :]

            uT = [T_("uT%d" % i, [128, KD, 512], BF16) for i in range(2)]
            szT = T_("szT", [128, 4, 512], BF16)
            pre = T_("pre", [128, 6, 515], F32)
            cacc = T_("cacc", [128, 6, 512], F32)
            xbcT = T_("xbcT", [128, 6, 512], BF16)
            dtT = T_("dtT", [8, 512], F32)
            dtmp = T_("dtmp", [8, 512], F32)
            gT_sb = [T_("gT%d" % i, [128, 4, 512], BF16) for i in range(2)]
            x_sb = T_("x_sb", [128, 512], BF16); B_sb = T_("B_sb", [128, 128], BF16)
            sm = T_("sm", [128, 64], F32, nres=8)
            dt_sb = sm[:, 0:8]; a_sb = sm[:, 8:16]; cum_sb = sm[:, 16:24]; ecum = sm[:, 24:32]
            w_sb = sm[:, 32:40]; elast = sm[:, 40:48]; ssq = sm[:, 48:52]; d1 = sm[:, 56:64]
            R_dt, R_a, R_cum, R_ecum, R_w, R_el, R_ssq, R_d1 = sm.res
            ahl = T_("ahl", [128, 32], BF16)
            Lhi = T_("Lhi", [128, 8, 128], BF16); Llo = T_("Llo", [128, 8, 128], BF16)
            decT = T_("decT", [128, 8, 128], BF16); MT = T_("MT", [128, 8, 128], BF16)
            Gm = T_("Gm", [128, 128], BF16)
            xdt = T_("xdt", [128, 512], BF16); xD = T_("xD", [128, 512], BF16); xw = T_("xw", [128, 512], BF16)
            t1 = T_("t1", [128, 512], F32); yv = T_("yv", [128, 512], F32); gt = T_("gt", [128, 512], F32)
            gbf = T_("gbf", [128, 512], BF16); junk = T_("junk", [128, 512], BF16)
            Sst = T_("Sst", [128, 512], F32); Sbf = T_("Sbf", [128, 512], BF16)
            acc = [T_("acc%d" % i, [128, 512], F32, psum=True) for i in range(2)]
            b2 = T_("b2", [128, 512], F32, psum=True, nres=2)
            b3 = T_("b3", [128, 512], F32, psum=True, nres=6)
            seg = T_("seg", [128, 1024], F32, psum=True)
            Yp = T_("Yp", [128, 512], F32, psum=True)
            Yo = T_("Yo", [128, 512], F32, psum=True)
            b2bf = b2[:].bitcast(BF16)
            Rx = Rsz = b2.r
            b3bf = b3[:].bitcast(BF16)
            RB = RgT = RG = b3.r
            Rdt = Rcum = Rlast = acc[1].r
            pB = b3bf[:, 0:128]
            pgT = b3bf[:, 128:640].rearrange("p (a b) -> p a b", b=128)
            pdt = acc[1][:, 0:16]
            pcum = acc[1][:, 32:48]
            plast = acc[1][:, 64:80]
            pG = b3[:, 384:512]
            seg3 = seg[:].rearrange("p (h q) -> p h q", q=128)

            identb = self.c_ident_bf
            identf = self.c_ident_f
            uT_all = self.dram["uT_all"].ap()
            uT_meta = self.dram["uT_meta"].ap().rearrange("(c p) t -> p c t", p=128)
            g_send = self.dram["g_send"].ap()
            R_gs = self.dres["g_send"]

            sc_i = 0
            for b in range(2):
                k.op("dve", lambda e: e.memset(Sst[:], 0.0), writes=[Sst.r])
                k.op("dve", lambda e: e.memset(Sbf[:], 0.0), writes=[Sbf.r])
                k.op("dve", lambda e: e.memset(pre[:, :, 0:3], 0.0), writes=[pre.r])
                scs = [("meta", 0, NMETA)] + [("real", t0, 512) for t0 in range(0, S, 512)]
                for kind, t0, T in scs[:self.dbg.get("max_sc", 999)]:
                    ub = sc_i % 2
                    sc_i += 1
                    u = uT[ub]
                    if self.dbg.get("no_udma", 0):
                        pass
                    elif kind == "meta":
                        k.dma("sp", u[:, :, 0:T], uT_meta, reads=[self.dres["uT_meta"]], writes=[u.r])
                    else:
                        rank = 4 * b + t0 // TPC
                        c0 = t0 % TPC
                        srcap = uT_all[rank * D:(rank + 1) * D, c0:c0 + T].rearrange("(c p) t -> p c t", p=128)
                        k.dma("sp", u[:, :, 0:T], srcap, reads=[self.dres["uT_all"]], writes=[u.r])
                    for cc in range(self.dbg.get('cc_max', 10)):
                        M = 128
                        a_ = acc[cc % 2]
                        for c in range(KD):
                            k.op("pe", lambda e: e.matmul(a_[0:M, 0:T], lhsT=Wg[:, c, cc * 128:cc * 128 + M],
                                                          rhs=u[:, c, 0:T], start=(c == 0), stop=(c == KD - 1)),
                                 reads=[Wg.r, u.r], writes=[a_.r])
                        if cc < 4:
                            k.op("act", lambda e: e.activation(out=szT[:, cc, 0:T], in_=a_[:, 0:T], func=(AF.Silu if self.dbg.get('silu', 1) else AF.Copy)),
                                 reads=[a_.r], writes=[szT.r])
                        else:
                            k.op("act", lambda e: e.activation(out=pre[:, cc - 4, 3:3 + T], in_=a_[:, 0:T],
                                                               func=AF.Copy),
                                 reads=[a_.r], writes=[pre.r])
                    for j in range(self.dbg.get('conv_n', 6)):
                        k.op("dve", lambda e: e.tensor_scalar(out=cacc[:, j, 0:T], in0=pre[:, j, 0:T],
                                                              scalar1=cw[:, j, 0:1], scalar2=cb[:, j:j + 1],
                                                              op0=ALU.mult, op1=ALU.add),
                             reads=[pre.r, cw.r, cb.r], writes=[cacc.r])
                        for kk in range(1, 4):
                            k.op("dve", lambda e: e.scalar_tensor_tensor(out=cacc[:, j, 0:T], in0=pre[:, j, kk:kk + T],
                                                                         scalar=cw[:, j, kk:kk + 1],
                                                                         in1=cacc[:, j, 0:T],
                                                                         op0=ALU.mult, op1=ALU.add),
                                 reads=[pre.r, cw.r, cacc.r], writes=[cacc.r])
                    for j in range(self.dbg.get("nact", 6)):
                        k.op("act", lambda e: e.activation(out=xbcT[:, j, 0:T], in_=cacc[:, j, 0:T], func=(AF.Silu if self.dbg.get('silu', 1) else AF.Copy)),
                             reads=[cacc.r], writes=[xbcT.r])
                    if not self.dbg.get("no_hist", 0):
                        k.op("dve", lambda e: e.tensor_copy(out=pre[:, :, 0:3], in_=pre[:, :, T:T + 3]),
                             reads=[pre.r], writes=[pre.r])
                    gsb = gT_sb[ub]
                    for q0 in (range(0, T, 128) if self.dbg.get("chunks", 1) else []):
                        Q = min(128, T - q0)
                        cs = slice(q0, q0 + Q)
                        for j in range(4):
                            k.op("pe", lambda e: e.transpose(out=b2bf[0:Q, j * 128:(j + 1) * 128], in_=xbcT[:, j, cs],
                                                             identity=identb[:, :]),
                                 reads=[xbcT.r, identb.r], writes=[Rx])
                        for j in range(4):
                            k.op("pe", lambda e: e.transpose(out=b2bf[0:Q, 512 + j * 128:512 + (j + 1) * 128],
                                                             in_=szT[:, j, cs], identity=identb[:, :]),
                                 reads=[szT.r, identb.r], writes=[Rsz])
                        k.op("pe", lambda e: e.transpose(out=pB[0:Q, :], in_=xbcT[:, 4, cs], identity=identb[:, :]),
                             reads=[xbcT.r, identb.r], writes=[RB])
                        for c in range(KD):
                            k.op("pe", lambda e: e.matmul(pdt[0:Q, :], lhsT=u[:, c, cs], rhs=Wdt[:, c, :],
                                                          start=(c == 0), stop=(c == KD - 1)),
                                 reads=[Wdt.r, u.r], writes=[Rdt])
                        k.op("act", lambda e: e.activation(out=x_sb[0:Q, :], in_=b2bf[0:Q, 0:512], func=AF.Copy),
                             reads=[Rx], writes=[x_sb.r])
                        k.op("act", lambda e: e.activation(out=B_sb[0:Q, :], in_=pB[0:Q, :], func=AF.Copy),
                             reads=[RB], writes=[B_sb.r])
                        k.op("dve", lambda e: e.tensor_tensor(out=dt_sb[0:Q, :], in0=pdt[0:Q, 8:16], in1=dtb[0:Q, 0, :], op=ALU.add),
                             reads=[Rdt, dtb.r], writes=[R_dt])
                        k.op("act", lambda e: e.activation(out=d1[0:Q, :], in_=dt_sb[0:Q, :], func=AF.Exp),
                             reads=[R_dt], writes=[R_d1])
                        k.op("act", lambda e: e.activation(out=dt_sb[0:Q, :], in_=d1[0:Q, :], func=AF.Ln,
                                                           bias=self.c_one[0:Q, 0:1]),
                             reads=[R_d1], writes=[R_dt])
                        k.op("dve", lambda e: e.tensor_tensor(out=a_sb[0:Q, :], in0=dt_sb[0:Q, :], in1=A2[0:Q, :],
                                                              op=ALU.mult),
                             reads=[R_dt, Abc.r], writes=[R_a])
                        k.op("dve", lambda e: e.tensor_copy(out=ahl[0:Q, 0:8], in_=a_sb[0:Q, :]),
                             reads=[R_a], writes=[ahl.r])
                        k.op("dve", lambda e: e.tensor_tensor(out=ahl[0:Q, 8:16], in0=a_sb[0:Q, :], in1=ahl[0:Q, 0:8],
                                                              op=ALU.subtract),
                             reads=[R_a, ahl.r], writes=[ahl.r])
                        k.op("dve", lambda e: e.tensor_copy(out=ahl[0:Q, 24:32], in_=ahl[0:Q, 0:8]),
                             reads=[ahl.r], writes=[ahl.r])
                        k.op("dve", lambda e: e.tensor_copy(out=ahl[0:Q, 16:24], in_=ahl[0:Q, 8:16]),
                             reads=[ahl.r], writes=[ahl.r])
                        for Lt, off in ((Lhi, 0), (Llo, 8)):
                            k.op("dve", lambda e: e.tensor_tensor(
                                out=Lt[0:Q, :, 0:Q],
                                in0=bc(sl_bf[0:Q, 0:Q].unsqueeze(1), [Q, 8, Q]),
                                in1=bc(ahl[0:Q, off:off + 8].unsqueeze(2), [Q, 8, Q]), op=ALU.mult),
                                 reads=[sl_bf.r, ahl.r], writes=[Lt.r])
                        for h in range(8):
                            k.op("pe", lambda e: e.matmul(seg3[0:Q, h, 0:Q], lhsT=Lhi[0:Q, h, 0:Q],
                                                          rhs=tri_bf[0:Q, 0:Q], start=True, stop=False),
                                 reads=[Lhi.r, tri_bf.r], writes=[seg.r])
                            k.op("pe", lambda e: e.matmul(seg3[0:Q, h, 0:Q], lhsT=Llo[0:Q, h, 0:Q],
                                                          rhs=tri_bf[0:Q, 0:Q], start=False, stop=True),
                                 reads=[Llo.r, tri_bf.r], writes=[seg.r])
                        for part in range(2):
                            k.op("pe", lambda e: e.matmul(pcum[0:Q, :], lhsT=tri_bf[0:Q, 0:Q],
                                                          rhs=ahl[0:Q, part * 16:part * 16 + 16],
                                                          start=(part == 0), stop=(part == 1)),
                                 reads=[tri_bf.r, ahl.r], writes=[Rcum])
                        for part in range(2):
                            k.op("pe", lambda e: e.matmul(plast[:, :], lhsT=ones_bf[0:Q, :],
                                                          rhs=ahl[0:Q, part * 16:part * 16 + 16],
                                                          start=(part == 0), stop=(part == 1)),
                                 reads=[ones_bf.r, ahl.r], writes=[Rlast])
                        k.op("pe", lambda e: e.matmul(pG[0:Q, 0:Q], lhsT=xbcT[:, 4, cs], rhs=xbcT[:, 5, cs],
                                                      start=True, stop=True),
                             reads=[xbcT.r], writes=[RG])
                        k.op("act", lambda e: e.activation(out=decT[0:Q, :, 0:Q], in_=seg3[0:Q, :, 0:Q], func=AF.Exp),
                             reads=[seg.r], writes=[decT.r])
                        k.op("dve", lambda e: e.tensor_tensor(out=Gm[0:Q, 0:Q], in0=pG[0:Q, 0:Q], in1=tri_bf[0:Q, 0:Q],
                                                              op=ALU.mult),
                             reads=[RG, tri_bf.r], writes=[Gm.r])
                        k.op("dve", lambda e: e.tensor_tensor(out=MT[0:Q, :, 0:Q], in0=decT[0:Q, :, 0:Q],
                                                              in1=bc(Gm[0:Q, 0:Q].unsqueeze(1), [Q, 8, Q]),
                                                              op=ALU.mult),
                             reads=[decT.r, Gm.r], writes=[MT.r])
                        x3 = x_sb[0:Q, :].rearrange("p (h d) -> p h d", d=64)
                        k.op("dve", lambda e: e.tensor_tensor(out=xdt[0:Q, :].rearrange("p (h d) -> p h d", d=64),
                                                              in0=x3, in1=bc(dt_sb[0:Q, :].unsqueeze(2), [Q, 8, 64]),
                                                              op=ALU.mult),
                             reads=[x_sb.r, R_dt], writes=[xdt.r])
                        k.op("dve", lambda e: e.tensor_tensor(out=xD[0:Q, :].rearrange("p (h d) -> p h d", d=64),
                                                              in0=x3, in1=bc(D2[0:Q, :].unsqueeze(2), [Q, 8, 64]),
                                                              op=ALU.mult),
                             reads=[x_sb.r, Dbc.r], writes=[xD.r])
                        k.op("pe", lambda e: e.matmul(Yp[0:Q, :], lhsT=identb[0:Q, 0:Q], rhs=xD[0:Q, :],
                                                      start=True, stop=False),
                             reads=[identb.r, xD.r], writes=[Yp.r])
                        for h in range(8):
                            k.op("pe", lambda e: e.matmul(Yp[0:Q, h * 64:(h + 1) * 64], lhsT=MT[0:Q, h, 0:Q],
                                                          rhs=xdt[0:Q, h * 64:(h + 1) * 64], start=False, stop=(h == 7)),
                                 reads=[MT.r, xdt.r], writes=[Yp.r])
                        k.op("pe", lambda e: e.matmul(Yo[0:Q, :], lhsT=xbcT[:, 5, cs], rhs=Sbf[:, :],
                                                      start=True, stop=True),
                             reads=[xbcT.r, Sbf.r], writes=[Yo.r])
                        k.op("act", lambda e: e.activation(out=cum_sb[0:Q, :], in_=pcum[0:Q, 0:8], func=AF.Copy),
                             reads=[Rcum], writes=[R_cum])
                        k.op("act", lambda e: e.activation(out=ecum[0:Q, :], in_=pcum[0:Q, 0:8], func=AF.Exp),
                             reads=[Rcum], writes=[R_ecum])
                        k.op("dve", lambda e: e.tensor_tensor(out=t1[0:Q, :].rearrange("p (h d) -> p h d", d=64),
                                                              in0=Yo[0:Q, :].rearrange("p (h d) -> p h d", d=64),
                                                              in1=bc(ecum[0:Q, :].unsqueeze(2), [Q, 8, 64]),
                                                              op=ALU.mult),
                             reads=[Yo.r, R_ecum], writes=[t1.r])
                        k.op("dve", lambda e: e.tensor_tensor(out=yv[0:Q, :], in0=Yp[0:Q, :], in1=t1[0:Q, :], op=ALU.add),
                             reads=[Yp.r, t1.r], writes=[yv.r])
                        k.op("dve", lambda e: e.tensor_tensor(out=gt[0:Q, :], in0=b2bf[0:Q, 512:1024], in1=yv[0:Q, :],
                                                              op=ALU.mult),
                             reads=[Rsz, yv.r], writes=[gt.r])
                        k.op("act", lambda e: e.activation(out=junk[0:Q, :], in_=gt[0:Q, :], func=AF.Square,
                                                           accum_out=ssq[0:Q, 0:1]),
                             reads=[gt.r], writes=[junk.r, R_ssq])
                        k.op("act", lambda e: e.activation(out=ssq[0:Q, 1:2], in_=ssq[0:Q, 0:1], func=AF.Ln,
                                                           scale=1.0 / 512, bias=self.c_eps[0:Q, 0:1]),
                             reads=[R_ssq], writes=[R_ssq])
                        k.op("act", lambda e: e.activation(out=ssq[0:Q, 2:3], in_=ssq[0:Q, 1:2], func=AF.Exp, scale=-0.5),
                             reads=[R_ssq], writes=[R_ssq])
                        k.op("dve", lambda e: e.scalar_tensor_tensor(out=gbf[0:Q, :], in0=gt[0:Q, :],
                                                                     scalar=ssq[0:Q, 2:3], in1=ng2[0:Q, :],
                                                                     op0=ALU.mult, op1=ALU.mult),
                             reads=[gt.r, R_ssq, ngb.r], writes=[gbf.r])
                        for j in range(4):
                            k.op("pe", lambda e: e.transpose(out=pgT[:, j, 0:Q], in_=gbf[0:Q, j * 128:(j + 1) * 128],
                                                             identity=identb[0:Q, 0:Q]),
                                 reads=[gbf.r, identb.r], writes=[RgT])
                        k.op("act", lambda e: e.activation(out=gsb[:, :, cs], in_=pgT[:, :, 0:Q], func=AF.Copy),
                             reads=[RgT], writes=[gsb.r])
                        k.op("dve", lambda e: e.tensor_tensor(out=d1[0:Q, :], in0=plast[0:Q, 0:8], in1=cum_sb[0:Q, :],
                                                              op=ALU.subtract),
                             reads=[Rlast, R_cum], writes=[R_d1])
                        k.op("act", lambda e: e.activation(out=w_sb[0:Q, :], in_=d1[0:Q, :], func=AF.Exp),
                             reads=[R_d1], writes=[R_w])
                        k.op("dve", lambda e: e.tensor_tensor(out=w_sb[0:Q, :], in0=w_sb[0:Q, :], in1=dt_sb[0:Q, :],
                                                              op=ALU.mult),
                             reads=[R_w, R_dt], writes=[R_w])
                        k.op("dve", lambda e: e.tensor_tensor(out=xw[0:Q, :].rearrange("p (h d) -> p h d", d=64),
                                                              in0=x3, in1=bc(w_sb[0:Q, :].unsqueeze(2), [Q, 8, 64]),
                                                              op=ALU.mult),
                             reads=[x_sb.r, R_w], writes=[xw.r])
                        k.op("act", lambda e: e.activation(out=elast[:, :], in_=plast[:, 0:8], func=AF.Exp),
                             reads=[Rlast], writes=[R_el])
                        k.op("pe", lambda e: e.matmul(Yo[:, :], lhsT=B_sb[0:Q, :], rhs=xw[0:Q, :], start=True, stop=True),
                             reads=[B_sb.r, xw.r], writes=[Yo.r])
                        S3 = Sst[:, :].rearrange("p (h d) -> p h d", d=64)
                        k.op("dve", lambda e: e.tensor_tensor(out=S3, in0=S3, in1=bc(elast[:, :].unsqueeze(2), [128, 8, 64]),
                                                              op=ALU.mult),
                             reads=[Sst.r, R_el], writes=[Sst.r])
                        k.op("dve", lambda e: e.tensor_tensor(out=Sst[:, :], in0=Yo[:, :], in1=Sst[:, :], op=ALU.add),
                             reads=[Yo.r, Sst.r], writes=[Sst.r])
                        k.op("act", lambda e: e.activation(out=Sbf[:, :], in_=Sst[:, :], func=AF.Copy),
                             reads=[Sst.r], writes=[Sbf.r])
                    if self.dbg.get("no_gsend", 0):
                        pass
                    elif kind == "meta":
                        if b == 0:
                            for r in range(NCORES):
                                k.dma("sp", g_send[r, :, TPC:TPC + NMETA].rearrange("(j p) t -> p j t", p=128),
                                      gsb[:, :, 0:NMETA], reads=[gsb.r], writes=[R_gs])
                    else:
                        rank = 4 * b + t0 // TPC
                        c0 = t0 % TPC
                        k.dma("sp", g_send[rank, :, c0:c0 + T].rearrange("(j p) t -> p j t", p=128),
                              gsb[:, :, 0:T], reads=[gsb.r], writes=[R_gs])
            k.barrier()
        if self.fused:
            self.dint("g_all", [NCORES * NCORES, 512, TPC + NMETA], BF16)
            k.collective("AllGather", self.dram["g_send"].ap().opt(), self.dram["g_all"].ap().opt(),
                         [list(range(NCORES))], reads=[self.dres["g_send"]], writes=[self.dres["g_all"]])

    WSPEC = {"op": (16, 32), "gu0": (88, 16), "dn0": (16, 44), "wo": (16, 16), "gu1": (88, 16), "dn1": (16, 44)}

    def phase5(self):
        k = self.k
        if getattr(self, "_w_done", False):
            return
        self._w_done = True
        for name, (CB, KC) in self.WSPEC.items():
            cbs = CB // NCORES
            n = cbs * 128 * KC * 128
            ws = self.din("ws_" + name, [n // 2048, 2048], F32)
            wb = self.buf("wsb_" + name, [n // 2048, 2048], BF16, 5, 5)
            k.dma("pool", wb.ap(), ws.ap(), writes=[self.dres["wsb_" + name]])
            if self.fused:
                full = self.dint("Wt_" + name, [NCORES * (n // 2048), 2048], BF16)
                k.collective("AllGather", wb.ap().opt(), full.ap().opt(), [list(range(NCORES))],
                             reads=[self.dres["wsb_" + name]], writes=[self.dres["Wt_" + name]])

    def wblock(self, name, cb):
        CB, KC = self.WSPEC[name]
        t = self.dram["Wt_" + name]
        per = 128 * KC * 128
        flat = t.ap().rearrange("a b -> (a b)")
        return flat[cb * per:(cb + 1) * per].rearrange("(p k j) -> p k j", p=128, k=KC)

    def token_phase(self, layer):
        k, nc, S, TPC = self.k, self.nc, self.S, self.TPC
        ph = 2 if layer == 0 else 4
        wm, wgu, wdn = ("op", "gu0", "dn0") if layer == 0 else ("wo", "gu1", "dn1")
        for nm in (wm, wgu, wdn):
            CB, KC = self.WSPEC[nm]
            self.buf("Wt_" + nm, [CB * 128 * KC * 128 // 2048, 2048], BF16, 5, ph)
        KM = self.WSPEC[wm][1]
        if layer == 0:
            x_loc = self.din("x_loc", [TPC, D]).ap()
            meta = self.din("meta", [NMETA, D]).ap()
            if self.fused:
                pid = nc.sync.partition_id()
                gsrc = self.dram["g_all"].ap().rearrange("(s d) c t -> d s c t", d=NCORES)
                gml = self.dint("g_mine", [NCORES, 512, TPC + NMETA], BF16)
                k.dma("sp", gml.ap(), gsrc[bass.ds(pid, 1), :, :, :].rearrange("a s c t -> (a s) c t"),
                      reads=[self.dres["g_all"]], writes=[self.dres["g_mine"]])
                gm = gml.ap()
                acts_src = lambda lo, w: [gm[:, :, lo:lo + w].rearrange("s (j p) t -> p (s j) t", p=128)]
                R_acts = self.dres["g_mine"]
            else:
                gm = self.din("g_mine", [NCORES, 512, TPC + NMETA], BF16).ap()
                acts_src = lambda lo, w: [gm[:, :, lo:lo + w].rearrange("s (j p) t -> p (s j) t", p=128)]
                R_acts = self.dres["g_mine"]
            hn_loc = self.buf("hn_loc", [D, TPC], BF16, 2, 2).ap().rearrange("(c p) t -> p c t", p=128)
            hn_meta = self.buf("hn_meta", [D, NMETA], BF16, 2, 2).ap().rearrange("(c p) t -> p c t", p=128)
            h1T = self.buf("h1T", [D, TPC + NMETA], F32, 2, 2).ap().rearrange("(c p) t -> p c t", p=128)
        else:
            if self.fused:
                pid = nc.sync.partition_id()
                osrc = self.dram["o_all"].ap().rearrange("(s d) c t -> d s c t", d=NCORES)
                oml = self.dint("o_mine", [NCORES, 256, TPC + NMETA], BF16)
                k.dma("sp", oml.ap(), osrc[bass.ds(pid, 1), :, :, :].rearrange("a s c t -> (a s) c t"),
                      reads=[self.dres["o_all"]], writes=[self.dres["o_mine"]])
                om = oml.ap()
                acts_src = lambda lo, w: [om[:, :, lo:lo + w].rearrange("s (j p) t -> p (s j) t", p=128)]
                R_acts = self.dres["o_mine"]
            else:
                om = self.din("o_mine", [NCORES, 256, TPC + NMETA], BF16).ap()
                acts_src = lambda lo, w: [om[:, :, lo:lo + w].rearrange("s (j p) t -> p (s j) t", p=128)]
                R_acts = self.dres["o_mine"]
            h1T = self.buf("h1T", [D, TPC + NMETA], F32, 2, 4).ap().rearrange("(c p) t -> p c t", p=128)
            out = self.dout("out", [TPC, D], F32).ap()
        gi_ffn = 1 if layer == 0 else 3
        identb = self.c_ident_bf
        W = 528
        with contextlib.ExitStack() as st:
            T_ = lambda n, s, d, **kw: self.tile(st, "t%d_" % layer + n, s, d, **kw)
            hT = T_("hT", [128, KD, W], F32)
            uT = T_("uT", [128, KD, W], BF16)
            rstd = T_("rstd", [128, W], F32)
            onesb = T_("onesb", [128, 128], BF16)
            k.op("dve", lambda e: e.memset(onesb[:], 1.0), writes=[onesb.r])
            A = [T_("A%d" % i, [128, 512], F32, psum=True) for i in range(2)]
            Mb = [T_("M%d" % i, [128, 512], F32, psum=True) for i in range(2)]
            U0 = T_("U0", [128, 512], F32, psum=True)
            UM = T_("UM", [128, 512], F32, psum=True)
            X = T_("X", [128, 512], F32, psum=True)
            cnt = [0]

            def gemm(wname, cb, KC, acts, segs, wtile, extra=None):
                k.dma("sp", wtile[:, 0:KC, :], self.wblock(wname, cb), reads=[self.dres["Wt_" + wname]],
                      writes=[wtile.r])
                outs = []
                for si, (lo, w) in enumerate(segs):
                    if extra is None:
                        p = (A if si == 0 else Mb)[cnt[0] % 2]
                    else:
                        p = extra[si]
                    for kk in range(KC):
                        k.op("pe", lambda e: e.matmul(p[:, 0:w], lhsT=wtile[:, kk, :], rhs=acts[:, kk, lo:lo + w],
                                                      start=(kk == 0), stop=(kk == KC - 1)),
                             reads=[wtile.r, acts.r], writes=[p.r])
                    outs.append((p, lo, w))
                if extra is None:
                    cnt[0] += 1
                return outs

            def stats(segs, scratch):
                for c in range(KD):
                    for (lo, w) in segs:
                        k.op("act", lambda e: e.activation(out=scratch[:, c, lo:lo + w], in_=hT[:, c, lo:lo + w],
                                                           func=AF.Square),
                             reads=[hT.r], writes=[scratch.r])
                for si, (lo, w) in enumerate(segs):
                    p = X if si == 0 else UM
                    for c in range(KD):
                        k.op("pe", lambda e: e.matmul(p[:, 0:w], lhsT=onesb[:, :], rhs=scratch[:, c, lo:lo + w],
                                                      start=(c == 0), stop=(c == KD - 1)),
                             reads=[onesb.r, scratch.r], writes=[p.r])
                    k.op("act", lambda e: e.activation(out=rstd[:, lo:lo + w], in_=p[:, 0:w], func=AF.Ln,
                                                       scale=1.0 / D, bias=self.c_eps[:, 0:1]),
                         reads=[p.r], writes=[rstd.r])
                    k.op("act", lambda e: e.activation(out=rstd[:, lo:lo + w], in_=rstd[:, lo:lo + w], func=AF.Exp,
                                                       scale=-0.5),
                         reads=[rstd.r], writes=[rstd.r])

            for ti in range(self.NT):
                t0 = ti * 512
                segs = [(0, 512)]
                if layer == 0 and ti == self.NT - 1:
                    segs.append((512, NMETA))
                if layer == 0:
                    with contextlib.ExitStack() as s1:
                        xt = [self.tile(s1, ("t0_xt%d" % i) + "_p%d" % ti, [128, D], F32) for i in range(2)]
                        xh = [self.tile(s1, ("t0_xh%d" % i) + "_p%d" % ti, [128, D], BF16) for i in range(5)]
                        xl = [self.tile(s1, ("t0_xl%d" % i) + "_p%d" % ti, [128, D], BF16) for i in range(5)]
                        subs = [(x_loc[t0 + j * 128:t0 + (j + 1) * 128, :], 128, j * 128) for j in range(4)]
                        if len(segs) > 1:
                            subs.append((meta[:, :], NMETA, 512))
                        for j, (srcap, n, col) in enumerate(subs):
                            b_ = j % 2
                            k.dma("sp", xt[b_][0:n, :], srcap, writes=[xt[b_].r])
                            k.op("dve", lambda e: e.tensor_copy(out=xh[j][0:n, :], in_=xt[b_][0:n, :]),
                                 reads=[xt[b_].r], writes=[xh[j].r])
                            k.op("dve", lambda e: e.tensor_tensor(out=xl[j][0:n, :], in0=xt[b_][0:n, :],
                                                                  in1=xh[j][0:n, :], op=ALU.subtract),
                                 reads=[xt[b_].r, xh[j].r], writes=[xl[j].r])
                        for c in range(KD):
                            p = A[c % 2]
                            for j, (srcap, n, col) in enumerate(subs):
                                pp = p if j < 4 else Mb[c % 2]
                                oc = col if j < 4 else 0
                                for part, xx in enumerate((xh, xl)):
                                    k.op("pe", lambda e: e.matmul(pp[:, oc:oc + n], lhsT=xx[j][0:n, c * 128:(c + 1) * 128],
                                                                  rhs=identb[0:n, 0:n], start=(part == 0), stop=(part == 1)),
                                         reads=[xx[j].r, identb.r], writes=[pp.r])
                            k.op("act", lambda e: e.activation(out=hT[:, c, 0:512], in_=p[:, 0:512], func=AF.Copy),
                                 reads=[p.r], writes=[hT.r])
                            if len(segs) > 1:
                                k.op("act", lambda e: e.activation(out=hT[:, c, 512:528], in_=Mb[c % 2][:, 0:NMETA],
                                                                   func=AF.Copy),
                                     reads=[Mb[c % 2].r], writes=[hT.r])
                        k.barrier()
                else:
                    k.dma("sp", hT[:, :, 0:512], h1T[:, :, t0:t0 + 512], reads=[self.dres["h1T"]], writes=[hT.r])
                with contextlib.ExitStack() as s2:
                    acts = self.tile(s2, ("t%d_acts" % layer) + "_p%d" % ti, [128, KM, W], BF16)
                    wt = [self.tile(s2, "t%d_wm%d_p%d" % (layer, i, ti), [128, KM, 128], BF16) for i in range(3)]
                    def load_acts(lo, w, dlo):
                        srcs = acts_src(lo, w)
                        per = KM // len(srcs)
                        for i_, s_ap in enumerate(srcs):
                            k.dma("sp", acts[:, i_ * per:(i_ + 1) * per, dlo:dlo + w], s_ap, reads=[R_acts],
                                  writes=[acts.r])
                    load_acts(t0, 512, 0)
                    if len(segs) > 1:
                        load_acts(TPC, NMETA, 512)
                    for cb in range(KD):
                        for (p, lo, w) in gemm(wm, cb, KM, acts, segs, wt[cb % 3]):
                            k.op("dve", lambda e: e.tensor_tensor(out=hT[:, cb, lo:lo + w], in0=p[:, 0:w],
                                                                  in1=hT[:, cb, lo:lo + w], op=ALU.add),
                                 reads=[p.r, hT.r], writes=[hT.r])
                    k.barrier()
                with contextlib.ExitStack() as s3:
                    sq = self.tile(s3, ("t%d_sq" % layer) + "_p%d" % ti, [128, KD, W], BF16)
                    stats(segs, sq)
                    for c in range(KD):
                        for (lo, w) in segs:
                            k.op("dve", lambda e: e.scalar_tensor_tensor(
                                out=uT[:, c, lo:lo + w], in0=hT[:, c, lo:lo + w], scalar=self.c_gains[:, gi_ffn, c:c + 1],
                                in1=rstd[:, lo:lo + w], op0=ALU.mult, op1=ALU.mult),
                                 reads=[hT.r, rstd.r, self.c_gains.r], writes=[uT.r])
                    k.barrier()
                with contextlib.ExitStack() as s4:
                    actT = self.tile(s4, ("t%d_actT" % layer) + "_p%d" % ti, [128, KF, W], BF16)
                    wg_ = [self.tile(s4, "t%d_wg%d_p%d" % (layer, i, ti), [128, KD, 128], BF16) for i in range(3)]
                    wu_ = [self.tile(s4, "t%d_wu%d_p%d" % (layer, i, ti), [128, KD, 128], BF16) for i in range(3)]
                    wd_ = [self.tile(s4, "t%d_wd%d_p%d" % (layer, i, ti), [128, KF, 128], BF16) for i in range(3)]
                    sg = [self.tile(s4, "t%d_sg%d_p%d" % (layer, i, ti), [128, W], F32) for i in range(2)]
                    for j in range(KF):
                        gs = gemm(wgu, j, KD, uT, segs, wg_[j % 3])
                        us = gemm(wgu, KF + j, KD, uT, segs, wu_[j % 3], extra=[U0, UM])
                        for (pg, lo, w), (pu, _, _) in zip(gs, us):
                            k.op("act", lambda e: e.activation(out=sg[j % 2][:, lo:lo + w], in_=pg[:, 0:w], func=AF.Silu),
                                 reads=[pg.r], writes=[sg[j % 2].r])
                            k.op("dve", lambda e: e.tensor_tensor(out=actT[:, j, lo:lo + w], in0=pu[:, 0:w],
                                                                  in1=sg[j % 2][:, lo:lo + w], op=ALU.mult),
                                 reads=[pu.r, sg[j % 2].r], writes=[actT.r])
                    for cb in range(KD):
                        for (p, lo, w) in gemm(wdn, cb, KF, actT, segs, wd_[cb % 3]):
                            k.op("dve", lambda e: e.tensor_tensor(out=hT[:, cb, lo:lo + w], in0=p[:, 0:w],
                                                                  in1=hT[:, cb, lo:lo + w], op=ALU.add),
                                 reads=[p.r, hT.r], writes=[hT.r])
                    k.barrier()
                with contextlib.ExitStack() as s5:
                    sq = self.tile(s5, ("t%d_sq5" % layer) + "_p%d" % ti, [128, KD, W], BF16)
                    stats(segs, sq)
                    if layer == 0:
                        hn = self.tile(s5, ("t0_hn") + "_p%d" % ti, [128, KD, W], BF16)
                        for c in range(KD):
                            for (lo, w) in segs:
                                k.op("dve", lambda e: e.tensor_tensor(out=hn[:, c, lo:lo + w], in0=hT[:, c, lo:lo + w],
                                                                      in1=rstd[:, lo:lo + w], op=ALU.mult),
                                     reads=[hT.r, rstd.r], writes=[hn.r])
                        k.dma("sp", hn_loc[:, :, t0:t0 + 512], hn[:, :, 0:512], reads=[hn.r],
                              writes=[self.dres["hn_loc"]])
                        k.dma("sp", h1T[:, :, t0:t0 + 512], hT[:, :, 0:512], reads=[hT.r], writes=[self.dres["h1T"]])
                        if len(segs) > 1:
                            k.dma("sp", hn_meta, hn[:, :, 512:528], reads=[hn.r], writes=[self.dres["hn_meta"]])
                            k.dma("sp", h1T[:, :, TPC:TPC + NMETA], hT[:, :, 512:528], reads=[hT.r],
                                  writes=[self.dres["h1T"]])
                    else:
                        yT = self.tile(s5, ("t1_yT") + "_p%d" % ti, [128, KD, 512], F32)
                        yh = self.tile(s5, ("t1_yh") + "_p%d" % ti, [128, KD, 512], BF16)
                        yl = self.tile(s5, ("t1_yl") + "_p%d" % ti, [128, KD, 512], BF16)
                        osb = [self.tile(s5, ("t1_osb%d" % i) + "_p%d" % ti, [128, D], F32) for i in range(2)]
                        for c in range(KD):
                            k.op("dve", lambda e: e.scalar_tensor_tensor(
                                out=yT[:, c, :], in0=hT[:, c, 0:512], scalar=self.c_gains[:, 5, c:c + 1],
                                in1=rstd[:, 0:512], op0=ALU.mult, op1=ALU.mult),
                                 reads=[hT.r, rstd.r, self.c_gains.r], writes=[yT.r])
                        k.op("dve", lambda e: e.tensor_copy(out=yh[:], in_=yT[:]), reads=[yT.r], writes=[yh.r])
                        k.op("dve", lambda e: e.tensor_tensor(out=yl[:], in0=yT[:], in1=yh[:], op=ALU.subtract),
                             reads=[yT.r, yh.r], writes=[yl.r])
                        for j in range(4):
                            o_ = osb[j % 2]
                            for c4 in range(4):
                                p = A[(j * 4 + c4) % 2]
                                for ci in range(4):
                                    c = c4 * 4 + ci
                                    for part, yy in enumerate((yh, yl)):
                                        k.op("pe", lambda e: e.matmul(p[:, ci * 128:(ci + 1) * 128],
                                                                      lhsT=yy[:, c, j * 128:(j + 1) * 128],
                                                                      rhs=identb[:, :], start=(part == 0), stop=(part == 1)),
                                             reads=[yy.r, identb.r], writes=[p.r])
                                k.op("act", lambda e: e.activation(out=o_[:, c4 * 512:(c4 + 1) * 512], in_=p[:, :],
                                                                   func=AF.Copy),
                                     reads=[p.r], writes=[o_.r])
                            k.dma("sp", out[t0 + j * 128:t0 + (j + 1) * 128, :], o_[:], reads=[o_.r],
                                  writes=[self.dres["out"]])
                    k.barrier()
            k.barrier()
        if layer == 0 and self.fused:
            self.dint("hn_all", [NCORES * D, TPC], BF16)
            k.collective("AllGather", self.dram["hn_loc"].ap().opt(), self.dram["hn_all"].ap().opt(),
                         [list(range(NCORES))], reads=[self.dres["hn_loc"]], writes=[self.dres["hn_all"]])

    def phase2(self):
        self.token_phase(0)

    def phase4(self):
        self.token_phase(1)

    def phase3(self):
        k, nc, S, TPC = self.k, self.nc, self.S, self.TPC
        self.din("wq_c", [D, 256]); self.din("wk_c", [D, 256]); self.din("wv_c", [D, 256])
        self.din("nltri_bf", [128, 128], BF16); self.din("nones_bf", [128, 128], BF16)
        self.din("dmask", [128, 4, 512], BF16)
        hn_all = self.buf("hn_all", [NCORES * D, TPC], BF16, 2, 3).ap()
        hn_meta = self.buf("hn_meta", [D, NMETA], BF16, 2, 3).ap().rearrange("(c p) t -> p c t", p=128)
        o_send = self.buf("o_send", [NCORES, 256, TPC + NMETA], BF16, 3, 3).ap()
        NKT = S // 128 + 1
        with contextlib.ExitStack() as st:
            T_ = lambda n, s, d, **kw: self.tile(st, "p3_" + n, s, d, **kw)
            Wq = T_("Wq", [128, KD, 256], BF16); Wk = T_("Wk", [128, KD, 256], BF16); Wv = T_("Wv", [128, KD, 256], BF16)
            nltri = T_("nltri", [128, 128], BF16); nones = T_("nones", [128, 128], BF16)
            dmask = T_("dmask", [128, 4, 512], BF16)
            for tl, nm in ((nltri, "nltri_bf"), (nones, "nones_bf"), (dmask, "dmask")):
                k.dma("sp", tl[:], self.dram[nm].ap(), writes=[tl.r])
            with contextlib.ExitStack() as s0:
                wtmp = self.tile(s0, "p3_wtmp", [128, KD, 256], F32)
                for Wt_, nm, gi, sc in ((Wq, "wq_c", 2, 128 ** -0.5), (Wk, "wk_c", 4, 1.0), (Wv, "wv_c", 4, 1.0)):
                    k.dma("sp", wtmp[:], self.dram[nm].ap().rearrange("(c p) n -> p c n", p=128), writes=[wtmp.r])
                    k.op("dve", lambda e: e.scalar_tensor_tensor(
                        out=Wt_[:], in0=wtmp[:], scalar=sc,
                        in1=bc(self.c_gains[:, gi, :].unsqueeze(2), [128, KD, 256]), op0=ALU.mult, op1=ALU.mult),
                         reads=[wtmp.r, self.c_gains.r], writes=[Wt_.r])
                k.barrier()
            qT = T_("qT", [128, 2, S], BF16)
            kT = T_("kT", [128, 2, S + NMETA], BF16)
            vs = T_("vs", [128, NKT, 256], BF16)
            hb = [T_("hb%d" % i, [128, KD, 512], BF16) for i in range(2)]
            e_sb = [T_("e%d" % i, [128, 512], F32) for i in range(2)]
            spb = [T_("sp%d" % i, [128, 512], BF16) for i in range(2)]
            wtl = [T_("w%d" % i, [128, 512], F32) for i in range(2)]
            att = [T_("att%d" % i, [128, 512], BF16) for i in range(2)]
            carry = T_("carry", [128, 512], F32)
            osb = [T_("osb%d" % i, [128, 512], BF16) for i in range(2)]
            PA = [T_("PA%d" % i, [128, 512], F32, psum=True) for i in range(2)]
            PW = [T_("PW%d" % i, [128, 512], F32, psum=True) for i in range(2)]
            PC = [T_("PC%d" % i, [128, 512], F32, psum=True) for i in range(2)]
            PO = T_("PO", [128, 512], F32, psum=True)
            sci = 0
            pa = 0
            for b in range(2):
                scs = [("meta", 0, NMETA)] + [("real", t0, 512) for t0 in range(0, S, 512)]
                for kind, t0, T in scs:
                    h_ = hb[sci % 2]; sci += 1
                    if kind == "meta":
                        k.dma("sp", h_[:, :, 0:T], hn_meta, reads=[self.dres["hn_meta"]], writes=[h_.r])
                    else:
                        rank = 4 * b + t0 // TPC
                        c0 = t0 % TPC
                        k.dma("sp", h_[:, :, 0:T],
                              hn_all[rank * D:(rank + 1) * D, c0:c0 + T].rearrange("(c p) t -> p c t", p=128),
                              reads=[self.dres["hn_all"]], writes=[h_.r])
                    for hd in range(2):
                        for Wt_, dst, off in ((Wk, kT, (0 if kind == "meta" else NMETA + t0)),) + \
                                (((Wq, qT, t0),) if kind == "real" else ()):
                            p = PA[pa % 2]; pa += 1
                            for c in range(KD):
                                k.op("pe", lambda e: e.matmul(p[:, 0:T], lhsT=Wt_[:, c, hd * 128:(hd + 1) * 128],
                                                              rhs=h_[:, c, 0:T], start=(c == 0), stop=(c == KD - 1)),
                                     reads=[Wt_.r, h_.r], writes=[p.r])
                            k.op("act", lambda e: e.activation(out=dst[:, hd, off:off + T], in_=p[:, 0:T], func=AF.Copy),
                                 reads=[p.r], writes=[dst.r])
                    for q0 in range(0, T, 128):
                        Q = min(128, T - q0)
                        tix = 0 if kind == "meta" else 1 + (t0 + q0) // 128
                        p = PA[pa % 2]; pa += 1
                        for c in range(KD):
                            k.op("pe", lambda e: e.matmul(p[0:Q, 0:256], lhsT=h_[:, c, q0:q0 + Q], rhs=Wv[:, c, :],
                                                          start=(c == 0), stop=(c == KD - 1)),
                                 reads=[Wv.r, h_.r], writes=[p.r])
                        k.op("act", lambda e: e.activation(out=vs[0:Q, tix, :], in_=p[0:Q, 0:256], func=AF.Copy),
                             reads=[p.r], writes=[vs.r])
                it = 0
                for hd in range(2):
                    for qb in range(S // 512):
                        qs = slice(qb * 512, (qb + 1) * 512)
                        k.op("dve", lambda e: e.memset(carry[:], 0.0), writes=[carry.r])
                        tiles = [("diag", 4 * qb + r, r) for r in (3, 2, 1, 0)] + \
                                [("full", kt, 0) for kt in range(4 * qb - 1, -1, -1)] + [("meta", -1, 0)]
                        for ti, (kind, kt, r) in enumerate(tiles):
                            Kp = NMETA if kind == "meta" else 128
                            kc0 = 0 if kind == "meta" else NMETA + kt * 128
                            tix = 0 if kind == "meta" else 1 + kt
                            last = (ti == len(tiles) - 1)
                            i2 = it % 2; it += 1
                            pw, pc = PW[i2], PC[i2]
                            k.op("pe", lambda e: e.matmul(pw[0:Kp, :], lhsT=kT[:, hd, kc0:kc0 + Kp], rhs=qT[:, hd, qs],
                                                          start=True, stop=False),
                                 reads=[kT.r, qT.r], writes=[pw.r])
                            k.op("act", lambda e: e.activation(out=e_sb[i2][0:Kp, :], in_=pw[0:Kp, :], func=AF.Exp),
                                 reads=[pw.r], writes=[e_sb[i2].r])
                            k.op("act", lambda e: e.activation(out=spb[i2][0:Kp, :], in_=e_sb[i2][0:Kp, :], func=AF.Ln,
                                                               bias=self.c_one[0:Kp, 0:1]),
                                 reads=[e_sb[i2].r], writes=[spb[i2].r])
                            if kind == "diag":
                                k.op("pool", lambda e: e.tensor_tensor(out=spb[i2][:, :], in0=spb[i2][:, :],
                                                                       in1=dmask[:, r, :], op=ALU.mult),
                                     reads=[spb[i2].r, dmask.r], writes=[spb[i2].r])
                            k.op("pe", lambda e: e.matmul(pw[0:Kp, :], lhsT=nltri[0:Kp, 0:Kp], rhs=spb[i2][0:Kp, :],
                                                          start=False, stop=True),
                                 reads=[nltri.r, spb[i2].r], writes=[pw.r])
                            if not last:
                                k.op("pe", lambda e: e.matmul(pc[:, :], lhsT=nones[0:Kp, :], rhs=spb[i2][0:Kp, :],
                                                              start=True, stop=True),
                                     reads=[nones.r, spb[i2].r], writes=[pc.r])
                            k.op("dve", lambda e: e.tensor_tensor(out=wtl[i2][0:Kp, :], in0=pw[0:Kp, :],
                                                                  in1=carry[0:Kp, :], op=ALU.add),
                                 reads=[pw.r, carry.r], writes=[wtl[i2].r])
                            if not last:
                                k.op("dve", lambda e: e.tensor_tensor(out=carry[:, :], in0=pc[:, :], in1=carry[:, :],
                                                                      op=ALU.add),
                                     reads=[pc.r, carry.r], writes=[carry.r])
                            k.op("act", lambda e: e.activation(out=att[i2][0:Kp, :], in_=wtl[i2][0:Kp, :], func=AF.Exp),
                                 reads=[wtl[i2].r], writes=[att[i2].r])
                            if kind == "diag":
                                k.op("pool", lambda e: e.tensor_tensor(out=att[i2][:, :], in0=att[i2][:, :],
                                                                       in1=dmask[:, r, :], op=ALU.mult),
                                     reads=[att[i2].r, dmask.r], writes=[att[i2].r])
                            k.op("pe", lambda e: e.matmul(PO[:, :], lhsT=vs[0:Kp, tix, hd * 128:(hd + 1) * 128],
                                                          rhs=att[i2][0:Kp, :], start=(ti == 0), stop=last),
                                 reads=[vs.r, att[i2].r], writes=[PO.r])
                        ob = osb[(hd * (S // 512) + qb) % 2]
                        k.op("act", lambda e: e.activation(out=ob[:, :], in_=PO[:, :], func=AF.Copy),
                             reads=[PO.r], writes=[ob.r])
                        rank = 4 * b + (qb * 512) // TPC
                        c0 = (qb * 512) % TPC
                        k.dma("sp", o_send[rank, hd * 128:(hd + 1) * 128, c0:c0 + 512], ob[:, :], reads=[ob.r],
                              writes=[self.dres["o_send"]])
            k.barrier()
        if self.fused:
            self.dint("o_all", [NCORES * NCORES, 256, TPC + NMETA], BF16)
            k.collective("AllGather", self.dram["o_send"].ap().opt(), self.dram["o_all"].ap().opt(),
                         [list(range(NCORES))], reads=[self.dres["o_send"]], writes=[self.dres["o_all"]])

    def build(self):
        k = self.k
        with contextlib.ExitStack() as st:
            self.load_consts(st)
            self.c_eps = self.tile(st, "c_eps", [128, 1], F32)
            k.op("dve", lambda e: e.memset(self.c_eps[:], EPS), writes=[self.c_eps.r])
            self.c_one = self.tile(st, "c_one", [128, 1], F32)
            k.op("dve", lambda e: e.memset(self.c_one[:], 1.0), writes=[self.c_one.r])
            k.barrier()
            for ph in self.phases:
                getattr(self, "phase%d" % ph)()
            outs = [self.dres[n] for n in self.ext_out]
            for name in self.dumps:
                t = self.dram[name]
                o = self.dout("dump_" + name, list(t.shape), t.dtype)
                k.dma("sp", o.ap(), t.ap(), reads=[self.dres[name]], writes=[self.dres["dump_" + name]])
                outs.append(self.dres["dump_" + name])
            k.final_wait(outs)
            k.barrier()
        return self.nc


def _bf(a):
    return np.asarray(a, dtype=np.float32).astype(ml_dtypes.bfloat16)


def _WFULL(inp):
    f = lambda a: np.asarray(a, np.float32)
    return {"op": f(inp["ssd_out_proj"][0]), "gu0": f(inp["ffn_gate_up"][0]), "dn0": f(inp["ffn_down"][0]),
            "wo": f(inp["sb_w_o"][0]), "gu1": f(inp["ffn_gate_up"][1]), "dn1": f(inp["ffn_down"][1])}


def make_in_maps(inp, S):
    TPC = S // 4
    x = np.asarray(inp["x"], np.float32)
    gains = np.stack([inp["norm_mix"][0], inp["norm_ffn"][0], inp["norm_mix"][1], inp["norm_ffn"][1],
                      inp["kv_norm"], inp["final_norm"]], 0).astype(np.float32)
    gains = np.ascontiguousarray(gains.reshape(6, KD, 128).transpose(2, 0, 1))
    maps = []
    for c in range(NCORES):
        b, s = c // 4, c % 4
        m = {
            "x_loc": np.ascontiguousarray(x[b, s * TPC:(s + 1) * TPC]),
            "meta": np.asarray(inp["meta_tokens"], np.float32),
            "gains": gains,
            "ident_bf": _bf(np.eye(128)),
            "ident_f32": np.eye(128, dtype=np.float32),
        }
        g = c
        wi = np.asarray(inp["ssd_in_proj"][0], np.float32)
        zc = wi[:, g * 512:(g + 1) * 512]
        xc = wi[:, DI + g * 512:DI + (g + 1) * 512]
        Bc = wi[:, 2 * DI + g * 128:2 * DI + (g + 1) * 128]
        Cc = wi[:, 2 * DI + 1024 + g * 128:2 * DI + 1024 + (g + 1) * 128]
        dc = wi[:, 2 * DI + 2048 + g * 8:2 * DI + 2048 + (g + 1) * 8]
        m["w_in_g"] = np.ascontiguousarray(np.concatenate([zc, xc, Bc, Cc, dc], 1))
        cwf = np.asarray(inp["ssd_conv_w"][0], np.float32)
        cbf = np.asarray(inp["ssd_conv_b"][0], np.float32)
        chans = np.concatenate([np.arange(g * 512, (g + 1) * 512), DI + g * 128 + np.arange(128),
                                DI + 1024 + g * 128 + np.arange(128)])
        m["conv_w"] = np.ascontiguousarray(cwf[:, chans].reshape(4, 6, 128).transpose(2, 1, 0))
        m["conv_b"] = np.ascontiguousarray(cbf[chans].reshape(6, 128).T)
        m["dt_bias"] = np.asarray(inp["ssd_dt_bias"][0], np.float32)[g * 8:(g + 1) * 8].reshape(1, 8)
        m["ones_bf"] = _bf(np.ones((128, 128)))
        hs = slice(c * 256, (c + 1) * 256)
        m["wq_c"] = np.ascontiguousarray(np.asarray(inp["sb_w_q"][0], np.float32)[:, hs])
        m["wk_c"] = np.ascontiguousarray(np.asarray(inp["w_kv"], np.float32)[:, hs])
        m["wv_c"] = np.ascontiguousarray(np.asarray(inp["w_kv"], np.float32)[:, D + c * 256:D + (c + 1) * 256])
        m["nltri_bf"] = _bf(-np.tril(np.ones((128, 128), np.float32)))
        m["nones_bf"] = _bf(-np.ones((128, 128)))
        jj = np.arange(128)[:, None, None]; rr = np.arange(4)[None, :, None]; tt = np.arange(512)[None, None, :]
        m["dmask"] = _bf((tt > jj + 128 * rr).astype(np.float32))
        for name, wfull in _WFULL(inp).items():
            CB, KC = MK.WSPEC[name]
            cbs = CB // NCORES
            cols = wfull[:, c * cbs * 128:(c + 1) * cbs * 128]
            til = cols.reshape(KC, 128, cbs, 128).transpose(2, 1, 0, 3)
            m["ws_" + name] = np.ascontiguousarray(til).reshape(-1, 2048)
        m["alog"] = np.asarray(inp["ssd_a_log"][0], np.float32)[g * 8:(g + 1) * 8].reshape(1, 8)
        m["dvec"] = np.asarray(inp["ssd_d"][0], np.float32)[g * 8:(g + 1) * 8].reshape(1, 8)
        m["ssd_norm"] = np.asarray(inp["ssd_norm"][0], np.float32)[g * 512:(g + 1) * 512].reshape(1, 512)
        tri = np.triu(np.ones((128, 128), np.float32))
        m["tri_bf"] = _bf(tri)
        m["sl_bf"] = _bf(1.0 - tri)
        m["tri_f32"] = tri
        m["ones_f32"] = np.ones((128, 128), np.float32)
        maps.append(m)
    return maps


def host_reshard(ph, outs, state):
    for c in range(NCORES):
        for n, v in outs[c].items():
            state[c][n] = np.asarray(v)
    if ph == 0:
        allg = np.concatenate([np.asarray(outs[c]["uT_loc"]) for c in range(NCORES)], 0)
        for c in range(NCORES):
            state[c]["uT_all"] = allg
    if ph == 1:
        for c in range(NCORES):
            state[c]["g_mine"] = np.ascontiguousarray(
                np.stack([np.asarray(outs[s]["g_send"])[c] for s in range(NCORES)], 0))
    if ph == 2:
        allg = np.concatenate([np.asarray(outs[c]["hn_loc"]) for c in range(NCORES)], 0)
        for c in range(NCORES):
            state[c]["hn_all"] = allg
    if ph == 3:
        for c in range(NCORES):
            state[c]["o_mine"] = np.ascontiguousarray(
                np.stack([np.asarray(outs[s]["o_send"])[c] for s in range(NCORES)], 0))
    if ph == 5:
        for name in MK.WSPEC:
            full = np.concatenate([np.asarray(outs[c]["wsb_" + name]) for c in range(NCORES)], 0)
            for c in range(NCORES):
                state[c]["Wt_" + name] = full


FUSED = True


def _run(nc, maps):
    return run_bass_kernel_spmd(nc, maps, core_ids=list(range(NCORES))).results


def kernel(**inputs):
    inp = {k_: np.asarray(v) for k_, v in inputs.items()}
    S = inp["x"].shape[1]
    TPC = S // 4
    base = make_in_maps(inp, S)
    if FUSED:
        mk = MK(S, phases=(0, 1, 5, 2, 3, 4), fused=True)
        nc = mk.build()
        maps = [{n: base[c][n] for n in mk.ext_in} for c in range(NCORES)]
        outs = _run(nc, maps)
    else:
        state = [{} for _ in range(NCORES)]
        for ph in (5, 0, 1, 2, 3, 4):
            mk = MK(S, phases=[ph], fused=False)
            nc = mk.build()
            maps = [{n: (state[c][n] if n in state[c] else base[c][n]) for n in mk.ext_in} for c in range(NCORES)]
            outs = _run(nc, maps)
            host_reshard(ph, outs, state)
            for c in range(NCORES):
                for n in list(state[c]):
                    if ph == 2 and n in ("Wt_op", "Wt_gu0", "Wt_dn0", "g_mine", "g_send", "uT_all", "uT_loc"):
                        del state[c][n]
                    elif ph == 5 and n.startswith("wsb_"):
                        del state[c][n]
    out = np.empty((2, S, D), np.float32)
    for c in range(NCORES):
        out[c // 4, (c % 4) * TPC:(c % 4 + 1) * TPC] = np.asarray(outs[c]["out"], np.float32)
    return out
```

```python
import contextlib
import numpy as np
import ml_dtypes
import concourse.bass as bass
import concourse.mybir as mybir
from concourse.bass_utils import run_bass_kernel_spmd

F32 = mybir.dt.float32
BF16 = mybir.dt.bfloat16
AF = mybir.ActivationFunctionType
ALU = mybir.AluOpType

NCORES = 8
D = 2048
KD = D // 128
NMETA = 16
DI = 4096
GW = 1288
FF = 5632
KF = FF // 128
EPS = 1e-6


class Res:
    __slots__ = ("name", "w", "r")

    def __init__(self, name=""):
        self.name = name
        self.w = None
        self.r = []


class K:
    def __init__(self, nc, n_dma_sems=12):
        self.nc = nc
        self.engs = {"pe": nc.tensor, "dve": nc.vector, "act": nc.scalar,
                     "pool": nc.gpsimd, "sp": nc.sync}
        self.sems = {}
        self.cnt = {}
        self._ctx = contextlib.ExitStack()
        for n in self.engs:
            self.sems[n] = self._ctx.enter_context(nc.semaphore("s_" + n))
            self.cnt[n] = 0
        self.dma_pool = {}
        for q in ("sp", "pool", "act"):
            lst = []
            for i in range(n_dma_sems):
                key = "d_%s%d" % (q, i)
                self.sems[key] = self._ctx.enter_context(nc.semaphore(key))
                self.cnt[key] = 0
                lst.append(key)
            self.dma_pool[q] = lst
        self.dma_rr = {q: 0 for q in self.dma_pool}
        self.known = {n: {} for n in self.engs}
        self.n_wait = 0
        self.n_ins = 0
        self.limit = None

    def _wait(self, eng, tok):
        if tok is None:
            return
        key, val = tok
        kn = self.known[eng]
        if kn.get(key, 0) >= val:
            return
        self.engs[eng].wait_ge(self.sems[key], val)
        kn[key] = val
        self.n_wait += 1

    def _deps(self, eng, reads, writes, same):
        for r in reads:
            if r.w is not None and not (r.w[0] == eng and not same):
                self._wait(eng, r.w)
        for w in writes:
            if w.w is not None and w.w[0] != eng:
                self._wait(eng, w.w)
            for t in w.r:
                if t[0] != eng:
                    self._wait(eng, t)

    def _mark(self, tok, reads, writes):
        for w in writes:
            w.w = tok
            w.r = []
        for r in reads:
            if r in writes:
                continue
            r.r.append(tok)
            if len(r.r) > 16:
                best = {}
                for k_, v in r.r:
                    if best.get(k_, 0) < v:
                        best[k_] = v
                r.r = list(best.items())

    def op(self, eng, fn, reads=(), writes=(), same=None):
        if same is None:
            same = (eng != "pe")
        if self.limit is not None and self.n_ins >= self.limit:
            return None
        self._deps(eng, reads, writes, same)
        ins = fn(self.engs[eng])
        self.cnt[eng] += 1
        ins.then_inc(self.sems[eng], 1)
        tok = (eng, self.cnt[eng])
        self._mark(tok, reads, writes)
        self.n_ins += 1
        return tok

    def dma(self, q, out, in_, reads=(), writes=(), **kw):
        pool = self.dma_pool[q]
        key = pool[self.dma_rr[q] % len(pool)]
        self.dma_rr[q] += 1
        if self.cnt[key] > 0:
            self._wait(q, (key, self.cnt[key]))
        self._deps(q, reads, writes, True)
        ins = self.engs[q].dma_start(out=out, in_=in_, **kw)
        self.cnt[key] += 16
        ins.then_inc(self.sems[key], 16)
        tok = (key, self.cnt[key])
        self._mark(tok, reads, writes)
        self.n_ins += 1
        return tok

    def collective(self, kind, ins_ap, outs_ap, groups, reads=(), writes=()):
        eng = "pool"
        self._deps(eng, reads, writes, True)
        ins = self.nc.gpsimd.collective_compute(
            kind, ALU.bypass, replica_groups=groups, ins=[ins_ap], outs=[outs_ap])
        self.cnt[eng] += 1
        ins.then_inc(self.sems[eng], 1)
        tok = (eng, self.cnt[eng])
        self._mark(tok, reads, writes)
        return tok

    def barrier(self):
        toks = [(n, c) for n, c in self.cnt.items() if c > 0]
        for e in self.engs:
            for t in toks:
                if t[0] == e:
                    continue
                self._wait(e, t)

    def final_wait(self, resources):
        for e in self.engs:
            for r in resources:
                self._wait(e, r.w)


class Tl:
    def __init__(self, stack, nc, name, shape, dtype, psum=False, nres=1):
        f = nc.psum_tensor if psum else nc.sbuf_tensor
        self.t = stack.enter_context(f(name, list(shape), dtype))
        self.res = [Res(name + str(i)) for i in range(nres)]
        self.r = self.res[0]

    def __getitem__(self, idx):
        return self.t[idx]


def bc(ap, shape):
    return ap.broadcast_to(list(shape))


class MK:
    def __init__(self, S, dumps=(), phases=(0, 1, 2, 3, 4), fused=True, dbg=None):
        self.dbg = dbg or {}
        self.phases = list(phases)
        self.fused = fused
        self.ext_in = []
        self.S = S
        self.TPC = S // 4
        self.NT = self.TPC // 512
        self.dumps = list(dumps)
        nc = bass.Bass("TRN2", target_bir_lowering=False)
        self.nc = nc
        self.k = K(nc)
        self.k.limit = self.dbg.get('limit')
        self.dram = {}
        self.dres = {}
        self.ext_out = []

    def din(self, name, shape, dtype=F32):
        if name in self.dram:
            return self.dram[name]
        t = self.nc.dram_tensor(name, list(shape), dtype, kind="ExternalInput")
        self.dram[name] = t
        self.dres[name] = Res(name)
        self.ext_in.append(name)
        return t

    def buf(self, name, shape, dtype, producer, cur):
        if name in self.dram:
            return self.dram[name]
        if self.fused:
            return self.dint(name, shape, dtype)
        if producer == cur:
            return self.dout(name, shape, dtype)
        return self.din(name, shape, dtype)

    def dint(self, name, shape, dtype):
        t = self.nc.dram_tensor(name, list(shape), dtype)
        self.dram[name] = t
        self.dres[name] = Res(name)
        return t

    def dout(self, name, shape, dtype=F32):
        t = self.nc.dram_tensor(name, list(shape), dtype, kind="ExternalOutput")
        self.dram[name] = t
        self.dres[name] = Res(name)
        self.ext_out.append(name)
        return t

    def tile(self, st, name, shape, dtype, psum=False, nres=1):
        return Tl(st, self.nc, name, shape, dtype, psum, nres)

    def load_consts(self, st):
        k = self.k
        self.din("ident_bf", [128, 128], BF16)
        self.din("ident_f32", [128, 128])
        self.din("gains", [128, 6, KD])
        self.c_ident_bf = self.tile(st, "c_ident_bf", [128, 128], BF16)
        self.c_ident_f = self.tile(st, "c_ident_f", [128, 128], F32)
        self.c_gains = self.tile(st, "c_gains", [128, 6, KD], F32)
        k.dma("sp", self.c_ident_bf[:], self.dram["ident_bf"].ap(), writes=[self.c_ident_bf.r])
        k.dma("sp", self.c_ident_f[:], self.dram["ident_f32"].ap(), writes=[self.c_ident_f.r])
        k.dma("sp", self.c_gains[:], self.dram["gains"].ap(), writes=[self.c_gains.r])

    def phase0(self):
        k, nc, TPC = self.k, self.nc, self.TPC
        with contextlib.ExitStack() as st:
            xt = [self.tile(st, "p0_x%d" % i, [128, D], F32) for i in range(2)]
            junk = self.tile(st, "p0_junk", [128, D], BF16)
            un = [self.tile(st, "p0_un%d" % i, [128, D], BF16) for i in range(2)]
            ss = [self.tile(st, "p0_ss%d" % i, [128, 2], F32) for i in range(2)]
            pT = [self.tile(st, "p0_pT%d" % i, [128, KD, 128], BF16, psum=True) for i in range(2)]
            uT = [self.tile(st, "p0_uT%d" % i, [128, KD, 512], BF16) for i in range(2)]
            x_loc = self.din("x_loc", [TPC, D]).ap()
            meta = self.din("meta", [NMETA, D]).ap()
            uT_loc = self.buf("uT_loc", [D, TPC], BF16, 0, 0).ap().rearrange("(c p) t -> p c t", p=128)
            uT_meta = self.buf("uT_meta", [D, NMETA], BF16, 0, 0).ap().rearrange("(c p) t -> p c t", p=128)
            g0 = self.c_gains[:, 0, :]
            ntile = TPC // 128
            tiles = [(i * 128, 128, False) for i in range(ntile)] + [(0, NMETA, True)]
            for ti, (r0, n, is_meta) in enumerate(tiles):
                b = ti % 2
                src = meta[0:n, :] if is_meta else x_loc[r0:r0 + n, :]
                k.dma("sp", xt[b][0:n, :], src, writes=[xt[b].r])
                k.op("act", lambda e: e.activation(out=junk[0:n, :], in_=xt[b][0:n, :], func=AF.Square,
                                                   accum_out=ss[b][0:n, 0:1]),
                     reads=[xt[b].r], writes=[junk.r, ss[b].r])
                k.op("act", lambda e: e.activation(out=ss[b][0:n, 1:2], in_=ss[b][0:n, 0:1], func=AF.Ln,
                                                   scale=1.0 / D, bias=self.c_eps[0:n, 0:1]),
                     reads=[ss[b].r], writes=[ss[b].r])
                k.op("act", lambda e: e.activation(out=ss[b][0:n, 1:2], in_=ss[b][0:n, 1:2], func=AF.Exp,
                                                   scale=-0.5),
                     reads=[ss[b].r], writes=[ss[b].r])
                k.op("dve", lambda e: e.tensor_scalar(out=un[b][0:n, :], in0=xt[b][0:n, :],
                                                      scalar1=ss[b][0:n, 1:2], scalar2=None, op0=ALU.mult),
                     reads=[xt[b].r, ss[b].r], writes=[un[b].r])
                for c in range(KD):
                    k.op("pe", lambda e: e.transpose(out=pT[b][:, c, 0:n], in_=un[b][0:n, c * 128:(c + 1) * 128],
                                                     identity=self.c_ident_bf[0:n, 0:n]),
                         reads=[un[b].r, self.c_ident_bf.r], writes=[pT[b].r])
                if is_meta:
                    ub, col = (ti // 4 + 1) % 2, 0
                else:
                    ub, col = (ti // 4) % 2, (ti % 4) * 128
                k.op("dve", lambda e: e.tensor_tensor(out=uT[ub][:, :, col:col + n], in0=pT[b][:, :, 0:n],
                                                      in1=bc(g0.unsqueeze(2), [128, KD, n]), op=ALU.mult),
                     reads=[pT[b].r, self.c_gains.r], writes=[uT[ub].r])
                if is_meta:
                    k.dma("sp", uT_meta, uT[ub][:, :, 0:n], reads=[uT[ub].r], writes=[self.dres["uT_meta"]])
                elif ti % 4 == 3:
                    t0 = (ti // 4) * 512
                    k.dma("sp", uT_loc[:, :, t0:t0 + 512], uT[ub][:, :, :], reads=[uT[ub].r],
                          writes=[self.dres["uT_loc"]])
            k.barrier()
        if self.fused:
            self.dint("uT_all", [NCORES * D, TPC], BF16)
            k.collective("AllGather", self.dram["uT_loc"].ap().opt(), self.dram["uT_all"].ap().opt(),
                         [list(range(NCORES))], reads=[self.dres["uT_loc"]], writes=[self.dres["uT_all"]])


    def phase1(self):
        k, nc, S, TPC = self.k, self.nc, self.S, self.TPC
        self.din("w_in_g", [D, GW])
        self.din("conv_w", [128, 6, 4])
        self.din("conv_b", [128, 6])
        self.din("dt_bias", [1, 8])
        self.din("ones_bf", [128, 128], BF16)
        self.din("alog", [1, 8])
        self.din("dvec", [1, 8])
        self.din("ssd_norm", [1, 512])
        self.din("tri_bf", [128, 128], BF16)
        self.din("sl_bf", [128, 128], BF16)
        self.din("tri_f32", [128, 128])
        self.din("ones_f32", [128, 128])
        self.buf("uT_all", [NCORES * D, TPC], BF16, 0, 1)
        self.buf("uT_meta", [D, NMETA], BF16, 0, 1)
        self.buf("g_send", [NCORES, 512, TPC + NMETA], BF16, 1, 1)
        with contextlib.ExitStack() as st:
            T_ = lambda n, s, d, **kw: self.tile(st, "p1_" + n, s, d, **kw)
            Wg = T_("Wg", [128, KD, GW], BF16)
            k.dma("pool", Wg[:], self.dram["w_in_g"].ap().rearrange("(c p) n -> p c n", p=128), writes=[Wg.r])
            if self.fused and self.dbg.get("hoist_w", 1):
                self.phase5()
            cw = T_("cw", [128, 6, 4], F32); cb = T_("cb", [128, 6], F32)
            dtb = T_("dtb", [128, 1, 8], F32)
            ones_bf = T_("ones_bf", [128, 128], BF16)
            Abc = T_("Abc", [128, 1, 8], F32); Dbc = T_("Dbc", [128, 1, 8], F32)
            ngb = T_("ngb", [128, 1, 512], F32)
            tri_bf = T_("tri_bf", [128, 128], BF16); sl_bf = T_("sl_bf", [128, 128], BF16)
            tri_f = T_("tri_f", [128, 128], F32); ones_f = T_("ones_f", [128, 128], F32)
            for tl, nm in ((cw, "conv_w"), (cb, "conv_b"), (ones_bf, "ones_bf"), (tri_bf, "tri_bf"), (sl_bf, "sl_bf"),
                           (tri_f, "tri_f32"), (ones_f, "ones_f32")):
                k.dma("sp", tl[:], self.dram[nm].ap(), writes=[tl.r])
            for tl, nm in ((Abc, "alog"), (Dbc, "dvec"), (ngb, "ssd_norm"), (dtb, "dt_bias")):
                k.dma("sp", tl[:], self.dram[nm].ap().partition_broadcast(128), writes=[tl.r])
            k.op("act", lambda e: e.activation(out=Abc[:], in_=Abc[:], func=AF.Exp), reads=[Abc.r], writes=[Abc.r])
            k.op("dve", lambda e: e.tensor_scalar(out=Abc[:], in0=Abc[:], scalar1=-1.0, scalar2=None, op0=ALU.mult),
                 reads=[Abc.r], writes=[Abc.r])
            Wdt = T_("Wdt", [128, KD, 16], BF16)
            k.op("dve", lambda e: e.tensor_copy(out=Wdt[:], in_=Wg[:, :, 1272:1288]), reads=[Wg.r], writes=[Wdt.r])
            A2 = Abc[:, 0, :]
            D2 = Dbc[:, 0, :]
            ng2 = ngb[:, 0, :]

            uT = [T_("uT%d" % i, [128, KD, 512], BF16) for i in range(2)]
            szT = T_("szT", [128, 4, 512], BF16)
            pre = T_("pre", [128, 6, 515], F32)
            cacc = T_("cacc", [128, 6, 512], F32)
            xbcT = T_("xbcT", [128, 6, 512], BF16)
            dtT = T_("dtT", [8, 512], F32)
            dtmp = T_("dtmp", [8, 512], F32)
            gT_sb = [T_("gT%d" % i, [128, 4, 512], BF16) for i in range(2)]
            x_sb = T_("x_sb", [128, 512], BF16); B_sb = T_("B_sb", [128, 128], BF16)
            sm = T_("sm", [128, 64], F32, nres=8)
            dt_sb = sm[:, 0:8]; a_sb = sm[:, 8:16]; cum_sb = sm[:, 16:24]; ecum = sm[:, 24:32]
            w_sb = sm[:, 32:40]; elast = sm[:, 40:48]; ssq = sm[:, 48:52]; d1 = sm[:, 56:64]
            R_dt, R_a, R_cum, R_ecum, R_w, R_el, R_ssq, R_d1 = sm.res
            ahl = T_("ahl", [128, 32], BF16)
            Lhi = T_("Lhi", [128, 8, 128], BF16); Llo = T_("Llo", [128, 8, 128], BF16)
            decT = T_("decT", [128, 8, 128], BF16); MT = T_("MT", [128, 8, 128], BF16)
            Gm = T_("Gm", [128, 128], BF16)
            xdt = T_("xdt", [128, 512], BF16); xD = T_("xD", [128, 512], BF16); xw = T_("xw", [128, 512], BF16)
            t1 = T_("t1", [128, 512], F32); yv = T_("yv", [128, 512], F32); gt = T_("gt", [128, 512], F32)
            gbf = T_("gbf", [128, 512], BF16); junk = T_("junk", [128, 512], BF16)
            Sst = T_("Sst", [128, 512], F32); Sbf = T_("Sbf", [128, 512], BF16)
            acc = [T_("acc%d" % i, [128, 512], F32, psum=True) for i in range(2)]
            b2 = T_("b2", [128, 512], F32, psum=True, nres=2)
            b3 = T_("b3", [128, 512], F32, psum=True, nres=6)
            seg = T_("seg", [128, 1024], F32, psum=True)
            Yp = T_("Yp", [128, 512], F32, psum=True)
            Yo = T_("Yo", [128, 512], F32, psum=True)
            b2bf = b2[:].bitcast(BF16)
            Rx = Rsz = b2.r
            b3bf = b3[:].bitcast(BF16)
            RB = RgT = RG = b3.r
            Rdt = Rcum = Rlast = acc[1].r
            pB = b3bf[:, 0:128]
            pgT = b3bf[:, 128:640].rearrange("p (a b) -> p a b", b=128)
            pdt = acc[1][:, 0:16]
            pcum = acc[1][:, 32:48]
            plast = acc[1][:, 64:80]
            pG = b3[:, 384:512]
            seg3 = seg[:].rearrange("p (h q) -> p h q", q=128)

            identb = self.c_ident_bf
            identf = self.c_ident_f
            uT_all = self.dram["uT_all"].ap()
            uT_meta = self.dram["uT_meta"].ap().rearrange("(c p) t -> p c t", p=128)
            g_send = self.dram["g_send"].ap()
            R_gs = self.dres["g_send"]

            sc_i = 0
            for b in range(2):
                k.op("dve", lambda e: e.memset(Sst[:], 0.0), writes=[Sst.r])
                k.op("dve", lambda e: e.memset(Sbf[:], 0.0), writes=[Sbf.r])
                k.op("dve", lambda e: e.memset(pre[:, :, 0:3], 0.0), writes=[pre.r])
                scs = [("meta", 0, NMETA)] + [("real", t0, 512) for t0 in range(0, S, 512)]
                for kind, t0, T in scs[:self.dbg.get("max_sc", 999)]:
                    ub = sc_i % 2
                    sc_i += 1
                    u = uT[ub]
                    if self.dbg.get("no_udma", 0):
                        pass
                    elif kind == "meta":
                        k.dma("sp", u[:, :, 0:T], uT_meta, reads=[self.dres["uT_meta"]], writes=[u.r])
                    else:
                        rank = 4 * b + t0 // TPC
                        c0 = t0 % TPC
                        srcap = uT_all[rank * D:(rank + 1) * D, c0:c0 + T].rearrange("(c p) t -> p c t", p=128)
                        k.dma("sp", u[:, :, 0:T], srcap, reads=[self.dres["uT_all"]], writes=[u.r])
                    for cc in range(self.dbg.get('cc_max', 10)):
                        M = 128
                        a_ = acc[cc % 2]
                        for c in range(KD):
                            k.op("pe", lambda e: e.matmul(a_[0:M, 0:T], lhsT=Wg[:, c, cc * 128:cc * 128 + M],
                                                          rhs=u[:, c, 0:T], start=(c == 0), stop=(c == KD - 1)),
                                 reads=[Wg.r, u.r], writes=[a_.r])
                        if cc < 4:
                            k.op("act", lambda e: e.activation(out=szT[:, cc, 0:T], in_=a_[:, 0:T], func=(AF.Silu if self.dbg.get('silu', 1) else AF.Copy)),
                                 reads=[a_.r], writes=[szT.r])
                        else:
                            k.op("act", lambda e: e.activation(out=pre[:, cc - 4, 3:3 + T], in_=a_[:, 0:T],
                                                               func=AF.Copy),
                                 reads=[a_.r], writes=[pre.r])
                    for j in range(self.dbg.get('conv_n', 6)):
                        k.op("dve", lambda e: e.tensor_scalar(out=cacc[:, j, 0:T], in0=pre[:, j, 0:T],
                                                              scalar1=cw[:, j, 0:1], scalar2=cb[:, j:j + 1],
                                                              op0=ALU.mult, op1=ALU.add),
                             reads=[pre.r, cw.r, cb.r], writes=[cacc.r])
                        for kk in range(1, 4):
                            k.op("dve", lambda e: e.scalar_tensor_tensor(out=cacc[:, j, 0:T], in0=pre[:, j, kk:kk + T],
                                                                         scalar=cw[:, j, kk:kk + 1],
                                                                         in1=cacc[:, j, 0:T],
                                                                         op0=ALU.mult, op1=ALU.add),
                                 reads=[pre.r, cw.r, cacc.r], writes=[cacc.r])
                    for j in range(self.dbg.get("nact", 6)):
                        k.op("act", lambda e: e.activation(out=xbcT[:, j, 0:T], in_=cacc[:, j, 0:T], func=(AF.Silu if self.dbg.get('silu', 1) else AF.Copy)),
                             reads=[cacc.r], writes=[xbcT.r])
                    if not self.dbg.get("no_hist", 0):
                        k.op("dve", lambda e: e.tensor_copy(out=pre[:, :, 0:3], in_=pre[:, :, T:T + 3]),
                             reads=[pre.r], writes=[pre.r])
                    gsb = gT_sb[ub]
                    for q0 in (range(0, T, 128) if self.dbg.get("chunks", 1) else []):
                        Q = min(128, T - q0)
                        cs = slice(q0, q0 + Q)
                        for j in range(4):
                            k.op("pe", lambda e: e.transpose(out=b2bf[0:Q, j * 128:(j + 1) * 128], in_=xbcT[:, j, cs],
                                                             identity=identb[:, :]),
                                 reads=[xbcT.r, identb.r], writes=[Rx])
                        for j in range(4):
                            k.op("pe", lambda e: e.transpose(out=b2bf[0:Q, 512 + j * 128:512 + (j + 1) * 128],
                                                             in_=szT[:, j, cs], identity=identb[:, :]),
                                 reads=[szT.r, identb.r], writes=[Rsz])
                        k.op("pe", lambda e: e.transpose(out=pB[0:Q, :], in_=xbcT[:, 4, cs], identity=identb[:, :]),
                             reads=[xbcT.r, identb.r], writes=[RB])
                        for c in range(KD):
                            k.op("pe", lambda e: e.matmul(pdt[0:Q, :], lhsT=u[:, c, cs], rhs=Wdt[:, c, :],
                                                          start=(c == 0), stop=(c == KD - 1)),
                                 reads=[Wdt.r, u.r], writes=[Rdt])
                        k.op("act", lambda e: e.activation(out=x_sb[0:Q, :], in_=b2bf[0:Q, 0:512], func=AF.Copy),
                             reads=[Rx], writes=[x_sb.r])
                        k.op("act", lambda e: e.activation(out=B_sb[0:Q, :], in_=pB[0:Q, :], func=AF.Copy),
                             reads=[RB], writes=[B_sb.r])
                        k.op("dve", lambda e: e.tensor_tensor(out=dt_sb[0:Q, :], in0=pdt[0:Q, 8:16], in1=dtb[0:Q, 0, :], op=ALU.add),
                             reads=[Rdt, dtb.r], writes=[R_dt])
                        k.op("act", lambda e: e.activation(out=d1[0:Q, :], in_=dt_sb[0:Q, :], func=AF.Exp),
                             reads=[R_dt], writes=[R_d1])
                        k.op("act", lambda e: e.activation(out=dt_sb[0:Q, :], in_=d1[0:Q, :], func=AF.Ln,
                                                           bias=self.c_one[0:Q, 0:1]),
                             reads=[R_d1], writes=[R_dt])
                        k.op("dve", lambda e: e.tensor_tensor(out=a_sb[0:Q, :], in0=dt_sb[0:Q, :], in1=A2[0:Q, :],
                                                              op=ALU.mult),
                             reads=[R_dt, Abc.r], writes=[R_a])
                        k.op("dve", lambda e: e.tensor_copy(out=ahl[0:Q, 0:8], in_=a_sb[0:Q, :]),
                             reads=[R_a], writes=[ahl.r])
                        k.op("dve", lambda e: e.tensor_tensor(out=ahl[0:Q, 8:16], in0=a_sb[0:Q, :], in1=ahl[0:Q, 0:8],
                                                              op=ALU.subtract),
                             reads=[R_a, ahl.r], writes=[ahl.r])
                        k.op("dve", lambda e: e.tensor_copy(out=ahl[0:Q, 24:32], in_=ahl[0:Q, 0:8]),
                             reads=[ahl.r], writes=[ahl.r])
                        k.op("dve", lambda e: e.tensor_copy(out=ahl[0:Q, 16:24], in_=ahl[0:Q, 8:16]),
                             reads=[ahl.r], writes=[ahl.r])
                        for Lt, off in ((Lhi, 0), (Llo, 8)):
                            k.op("dve", lambda e: e.tensor_tensor(
                                out=Lt[0:Q, :, 0:Q],
                                in0=bc(sl_bf[0:Q, 0:Q].unsqueeze(1), [Q, 8, Q]),
                                in1=bc(ahl[0:Q, off:off + 8].unsqueeze(2), [Q, 8, Q]), op=ALU.mult),
                                 reads=[sl_bf.r, ahl.r], writes=[Lt.r])
                        for h in range(8):
                            k.op("pe", lambda e: e.matmul(seg3[0:Q, h, 0:Q], lhsT=Lhi[0:Q, h, 0:Q],
                                                          rhs=tri_bf[0:Q, 0:Q], start=True, stop=False),
                                 reads=[Lhi.r, tri_bf.r], writes=[seg.r])
                            k.op("pe", lambda e: e.matmul(seg3[0:Q, h, 0:Q], lhsT=Llo[0:Q, h, 0:Q],
                                                          rhs=tri_bf[0:Q, 0:Q], start=False, stop=True),
                                 reads=[Llo.r, tri_bf.r], writes=[seg.r])
                        for part in range(2):
                            k.op("pe", lambda e: e.matmul(pcum[0:Q, :], lhsT=tri_bf[0:Q, 0:Q],
                                                          rhs=ahl[0:Q, part * 16:part * 16 + 16],
                                                          start=(part == 0), stop=(part == 1)),
                                 reads=[tri_bf.r, ahl.r], writes=[Rcum])
                        for part in range(2):
                            k.op("pe", lambda e: e.matmul(plast[:, :], lhsT=ones_bf[0:Q, :],
                                                          rhs=ahl[0:Q, part * 16:part * 16 + 16],
                                                          start=(part == 0), stop=(part == 1)),
                                 reads=[ones_bf.r, ahl.r], writes=[Rlast])
                        k.op("pe", lambda e: e.matmul(pG[0:Q, 0:Q], lhsT=xbcT[:, 4, cs], rhs=xbcT[:, 5, cs],
                                                      start=True, stop=True),
                             reads=[xbcT.r], writes=[RG])
                        k.op("act", lambda e: e.activation(out=decT[0:Q, :, 0:Q], in_=seg3[0:Q, :, 0:Q], func=AF.Exp),
                             reads=[seg.r], writes=[decT.r])
                        k.op("dve", lambda e: e.tensor_tensor(out=Gm[0:Q, 0:Q], in0=pG[0:Q, 0:Q], in1=tri_bf[0:Q, 0:Q],
                                                              op=ALU.mult),
                             reads=[RG, tri_bf.r], writes=[Gm.r])
                        k.op("dve", lambda e: e.tensor_tensor(out=MT[0:Q, :, 0:Q], in0=decT[0:Q, :, 0:Q],
                                                              in1=bc(Gm[0:Q, 0:Q].unsqueeze(1), [Q, 8, Q]),
                                                              op=ALU.mult),
                             reads=[decT.r, Gm.r], writes=[MT.r])
                        x3 = x_sb[0:Q, :].rearrange("p (h d) -> p h d", d=64)
                        k.op("dve", lambda e: e.tensor_tensor(out=xdt[0:Q, :].rearrange("p (h d) -> p h d", d=64),
                                                              in0=x3, in1=bc(dt_sb[0:Q, :].unsqueeze(2), [Q, 8, 64]),
                                                              op=ALU.mult),
                             reads=[x_sb.r, R_dt], writes=[xdt.r])
                        k.op("dve", lambda e: e.tensor_tensor(out=xD[0:Q, :].rearrange("p (h d) -> p h d", d=64),
                                                              in0=x3, in1=bc(D2[0:Q, :].unsqueeze(2), [Q, 8, 64]),
                                                              op=ALU.mult),
                             reads=[x_sb.r, Dbc.r], writes=[xD.r])
                        k.op("pe", lambda e: e.matmul(Yp[0:Q, :], lhsT=identb[0:Q, 0:Q], rhs=xD[0:Q, :],
                                                      start=True, stop=False),
                             reads=[identb.r, xD.r], writes=[Yp.r])
                        for h in range(8):
                            k.op("pe", lambda e: e.matmul(Yp[0:Q, h * 64:(h + 1) * 64], lhsT=MT[0:Q, h, 0:Q],
                                                          rhs=xdt[0:Q, h * 64:(h + 1) * 64], start=False, stop=(h == 7)),
                                 reads=[MT.r, xdt.r], writes=[Yp.r])
                        k.op("pe", lambda e: e.matmul(Yo[0:Q, :], lhsT=xbcT[:, 5, cs], rhs=Sbf[:, :],
                                                      start=True, stop=True),
                             reads=[xbcT.r, Sbf.r], writes=[Yo.r])
                        k.op("act", lambda e: e.activation(out=cum_sb[0:Q, :], in_=pcum[0:Q, 0:8], func=AF.Copy),
                             reads=[Rcum], writes=[R_cum])
                        k.op("act", lambda e: e.activation(out=ecum[0:Q, :], in_=pcum[0:Q, 0:8], func=AF.Exp),
                             reads=[Rcum], writes=[R_ecum])
                        k.op("dve", lambda e: e.tensor_tensor(out=t1[0:Q, :].rearrange("p (h d) -> p h d", d=64),
                                                              in0=Yo[0:Q, :].rearrange("p (h d) -> p h d", d=64),
                                                              in1=bc(ecum[0:Q, :].unsqueeze(2), [Q, 8, 64]),
                                                              op=ALU.mult),
                             reads=[Yo.r, R_ecum], writes=[t1.r])
                        k.op("dve", lambda e: e.tensor_tensor(out=yv[0:Q, :], in0=Yp[0:Q, :], in1=t1[0:Q, :], op=ALU.add),
                             reads=[Yp.r, t1.r], writes=[yv.r])
                        k.op("dve", lambda e: e.tensor_tensor(out=gt[0:Q, :], in0=b2bf[0:Q, 512:1024], in1=yv[0:Q, :],
                                                              op=ALU.mult),
                             reads=[Rsz, yv.r], writes=[gt.r])
                        k.op("act", lambda e: e.activation(out=junk[0:Q, :], in_=gt[0:Q, :], func=AF.Square,
                                                           accum_out=ssq[0:Q, 0:1]),
                             reads=[gt.r], writes=[junk.r, R_ssq])
                        k.op("act", lambda e: e.activation(out=ssq[0:Q, 1:2], in_=ssq[0:Q, 0:1], func=AF.Ln,
                                                           scale=1.0 / 512, bias=self.c_eps[0:Q, 0:1]),
                             reads=[R_ssq], writes=[R_ssq])
                        k.op("act", lambda e: e.activation(out=ssq[0:Q, 2:3], in_=ssq[0:Q, 1:2], func=AF.Exp, scale=-0.5),
                             reads=[R_ssq], writes=[R_ssq])
                        k.op("dve", lambda e: e.scalar_tensor_tensor(out=gbf[0:Q, :], in0=gt[0:Q, :],
                                                                     scalar=ssq[0:Q, 2:3], in1=ng2[0:Q, :],
                                                                     op0=ALU.mult, op1=ALU.mult),
                             reads=[gt.r, R_ssq, ngb.r], writes=[gbf.r])
                        for j in range(4):
                            k.op("pe", lambda e: e.transpose(out=pgT[:, j, 0:Q], in_=gbf[0:Q, j * 128:(j + 1) * 128],
                                                             identity=identb[0:Q, 0:Q]),
                                 reads=[gbf.r, identb.r], writes=[RgT])
                        k.op("act", lambda e: e.activation(out=gsb[:, :, cs], in_=pgT[:, :, 0:Q], func=AF.Copy),
                             reads=[RgT], writes=[gsb.r])
                        k.op("dve", lambda e: e.tensor_tensor(out=d1[0:Q, :], in0=plast[0:Q, 0:8], in1=cum_sb[0:Q, :],
                                                              op=ALU.subtract),
                             reads=[Rlast, R_cum], writes=[R_d1])
                        k.op("act", lambda e: e.activation(out=w_sb[0:Q, :], in_=d1[0:Q, :], func=AF.Exp),
                             reads=[R_d1], writes=[R_w])
                        k.op("dve", lambda e: e.tensor_tensor(out=w_sb[0:Q, :], in0=w_sb[0:Q, :], in1=dt_sb[0:Q, :],
                                                              op=ALU.mult),
                             reads=[R_w, R_dt], writes=[R_w])
                        k.op("dve", lambda e: e.tensor_tensor(out=xw[0:Q, :].rearrange("p (h d) -> p h d", d=64),
                                                              in0=x3, in1=bc(w_sb[0:Q, :].unsqueeze(2), [Q, 8, 64]),
                                                              op=ALU.mult),
                             reads=[x_sb.r, R_w], writes=[xw.r])
                        k.op("act", lambda e: e.activation(out=elast[:, :], in_=plast[:, 0:8], func=AF.Exp),
                             reads=[Rlast], writes=[R_el])
                        k.op("pe", lambda e: e.matmul(Yo[:, :], lhsT=B_sb[0:Q, :], rhs=xw[0:Q, :], start=True, stop=True),
                             reads=[B_sb.r, xw.r], writes=[Yo.r])
                        S3 = Sst[:, :].rearrange("p (h d) -> p h d", d=64)
                        k.op("dve", lambda e: e.tensor_tensor(out=S3, in0=S3, in1=bc(elast[:, :].unsqueeze(2), [128, 8, 64]),
                                                              op=ALU.mult),
                             reads=[Sst.r, R_el], writes=[Sst.r])
                        k.op("dve", lambda e: e.tensor_tensor(out=Sst[:, :], in0=Yo[:, :], in1=Sst[:, :], op=ALU.add),
                             reads=[Yo.r, Sst.r], writes=[Sst.r])
                        k.op("act", lambda e: e.activation(out=Sbf[:, :], in_=Sst[:, :], func=AF.Copy),
                             reads=[Sst.r], writes=[Sbf.r])
                    if self.dbg.get("no_gsend", 0):
                        pass
                    elif kind == "meta":
                        if b == 0:
                            for r in range(NCORES):
                                k.dma("sp", g_send[r, :, TPC:TPC + NMETA].rearrange("(j p) t -> p j t", p=128),
                                      gsb[:, :, 0:NMETA], reads=[gsb.r], writes=[R_gs])
                    else:
                        rank = 4 * b + t0 // TPC
                        c0 = t0 % TPC
                        k.dma("sp", g_send[rank, :, c0:c0 + T].rearrange("(j p) t -> p j t", p=128),
                              gsb[:, :, 0:T], reads=[gsb.r], writes=[R_gs])
            k.barrier()
        if self.fused:
            self.dint("g_all", [NCORES * NCORES, 512, TPC + NMETA], BF16)
            k.collective("AllGather", self.dram["g_send"].ap().opt(), self.dram["g_all"].ap().opt(),
                         [list(range(NCORES))], reads=[self.dres["g_send"]], writes=[self.dres["g_all"]])

    WSPEC = {"op": (16, 32), "gu0": (88, 16), "dn0": (16, 44), "wo": (16, 16), "gu1": (88, 16), "dn1": (16, 44)}

    def phase5(self):
        k = self.k
        if getattr(self, "_w_done", False):
            return
        self._w_done = True
        for name, (CB, KC) in self.WSPEC.items():
            cbs = CB // NCORES
            n = cbs * 128 * KC * 128
            ws = self.din("ws_" + name, [n // 2048, 2048], F32)
            wb = self.buf("wsb_" + name, [n // 2048, 2048], BF16, 5, 5)
            k.dma("pool", wb.ap(), ws.ap(), writes=[self.dres["wsb_" + name]])
            if self.fused:
                full = self.dint("Wt_" + name, [NCORES * (n // 2048), 2048], BF16)
                k.collective("AllGather", wb.ap().opt(), full.ap().opt(), [list(range(NCORES))],
                             reads=[self.dres["wsb_" + name]], writes=[self.dres["Wt_" + name]])

    def wblock(self, name, cb):
        CB, KC = self.WSPEC[name]
        t = self.dram["Wt_" + name]
        per = 128 * KC * 128
        flat = t.ap().rearrange("a b -> (a b)")
        return flat[cb * per:(cb + 1) * per].rearrange("(p k j) -> p k j", p=128, k=KC)

    def token_phase(self, layer):
        k, nc, S, TPC = self.k, self.nc, self.S, self.TPC
        ph = 2 if layer == 0 else 4
        wm, wgu, wdn = ("op", "gu0", "dn0") if layer == 0 else ("wo", "gu1", "dn1")
        for nm in (wm, wgu, wdn):
            CB, KC = self.WSPEC[nm]
            self.buf("Wt_" + nm, [CB * 128 * KC * 128 // 2048, 2048], BF16, 5, ph)
        KM = self.WSPEC[wm][1]
        if layer == 0:
            x_loc = self.din("x_loc", [TPC, D]).ap()
            meta = self.din("meta", [NMETA, D]).ap()
            if self.fused:
                pid = nc.sync.partition_id()
                gsrc = self.dram["g_all"].ap().rearrange("(s d) c t -> d s c t", d=NCORES)
                gml = self.dint("g_mine", [NCORES, 512, TPC + NMETA], BF16)
                k.dma("sp", gml.ap(), gsrc[bass.ds(pid, 1), :, :, :].rearrange("a s c t -> (a s) c t"),
                      reads=[self.dres["g_all"]], writes=[self.dres["g_mine"]])
                gm = gml.ap()
                acts_src = lambda lo, w: [gm[:, :, lo:lo + w].rearrange("s (j p) t -> p (s j) t", p=128)]
                R_acts = self.dres["g_mine"]
            else:
                gm = self.din("g_mine", [NCORES, 512, TPC + NMETA], BF16).ap()
                acts_src = lambda lo, w: [gm[:, :, lo:lo + w].rearrange("s (j p) t -> p (s j) t", p=128)]
                R_acts = self.dres["g_mine"]
            hn_loc = self.buf("hn_loc", [D, TPC], BF16, 2, 2).ap().rearrange("(c p) t -> p c t", p=128)
            hn_meta = self.buf("hn_meta", [D, NMETA], BF16, 2, 2).ap().rearrange("(c p) t -> p c t", p=128)
            h1T = self.buf("h1T", [D, TPC + NMETA], F32, 2, 2).ap().rearrange("(c p) t -> p c t", p=128)
        else:
            if self.fused:
                pid = nc.sync.partition_id()
                osrc = self.dram["o_all"].ap().rearrange("(s d) c t -> d s c t", d=NCORES)
                oml = self.dint("o_mine", [NCORES, 256, TPC + NMETA], BF16)
                k.dma("sp", oml.ap(), osrc[bass.ds(pid, 1), :, :, :].rearrange("a s c t -> (a s) c t"),
                      reads=[self.dres["o_all"]], writes=[self.dres["o_mine"]])
                om = oml.ap()
                acts_src = lambda lo, w: [om[:, :, lo:lo + w].rearrange("s (j p) t -> p (s j) t", p=128)]
                R_acts = self.dres["o_mine"]
            else:
                om = self.din("o_mine", [NCORES, 256, TPC + NMETA], BF16).ap()
                acts_src = lambda lo, w: [om[:, :, lo:lo + w].rearrange("s (j p) t -> p (s j) t", p=128)]
                R_acts = self.dres["o_mine"]
            h1T = self.buf("h1T", [D, TPC + NMETA], F32, 2, 4).ap().rearrange("(c p) t -> p c t", p=128)
            out = self.dout("out", [TPC, D], F32).ap()
        gi_ffn = 1 if layer == 0 else 3
        identb = self.c_ident_bf
        W = 528
        with contextlib.ExitStack() as st:
            T_ = lambda n, s, d, **kw: self.tile(st, "t%d_" % layer + n, s, d, **kw)
            hT = T_("hT", [128, KD, W], F32)
            uT = T_("uT", [128, KD, W], BF16)
            rstd = T_("rstd", [128, W], F32)
            onesb = T_("onesb", [128, 128], BF16)
            k.op("dve", lambda e: e.memset(onesb[:], 1.0), writes=[onesb.r])
            A = [T_("A%d" % i, [128, 512], F32, psum=True) for i in range(2)]
            Mb = [T_("M%d" % i, [128, 512], F32, psum=True) for i in range(2)]
            U0 = T_("U0", [128, 512], F32, psum=True)
            UM = T_("UM", [128, 512], F32, psum=True)
            X = T_("X", [128, 512], F32, psum=True)
            cnt = [0]
            wq_ = [0]

            def gemm(wname, cb, KC, acts, segs, wtile, extra=None):
                wq_[0] += 1
                k.dma(("sp" if wq_[0] % 2 else "pool"), wtile[:, 0:KC, :], self.wblock(wname, cb),
                      reads=[self.dres["Wt_" + wname]], writes=[wtile.r])
                outs = []
                for si, (lo, w) in enumerate(segs):
                    if extra is None:
                        p = (A if si == 0 else Mb)[cnt[0] % 2]
                    else:
                        p = extra[si]
                    for kk in range(KC):
                        k.op("pe", lambda e: e.matmul(p[:, 0:w], lhsT=wtile[:, kk, :], rhs=acts[:, kk, lo:lo + w],
                                                      start=(kk == 0), stop=(kk == KC - 1)),
                             reads=[wtile.r, acts.r], writes=[p.r])
                    outs.append((p, lo, w))
                if extra is None:
                    cnt[0] += 1
                return outs

            def stats(segs, scratch):
                for c in range(KD):
                    for (lo, w) in segs:
                        k.op("act", lambda e: e.activation(out=scratch[:, c, lo:lo + w], in_=hT[:, c, lo:lo + w],
                                                           func=AF.Square),
                             reads=[hT.r], writes=[scratch.r])
                for si, (lo, w) in enumerate(segs):
                    p = X if si == 0 else UM
                    for c in range(KD):
                        k.op("pe", lambda e: e.matmul(p[:, 0:w], lhsT=onesb[:, :], rhs=scratch[:, c, lo:lo + w],
                                                      start=(c == 0), stop=(c == KD - 1)),
                             reads=[onesb.r, scratch.r], writes=[p.r])
                    k.op("act", lambda e: e.activation(out=rstd[:, lo:lo + w], in_=p[:, 0:w], func=AF.Ln,
                                                       scale=1.0 / D, bias=self.c_eps[:, 0:1]),
                         reads=[p.r], writes=[rstd.r])
                    k.op("act", lambda e: e.activation(out=rstd[:, lo:lo + w], in_=rstd[:, lo:lo + w], func=AF.Exp,
                                                       scale=-0.5),
                         reads=[rstd.r], writes=[rstd.r])

            for ti in range(self.NT):
                t0 = ti * 512
                segs = [(0, 512)]
                if layer == 0 and ti == self.NT - 1:
                    segs.append((512, NMETA))
                if layer == 0:
                    with contextlib.ExitStack() as s1:
                        xt = [self.tile(s1, ("t0_xt%d" % i) + "_p%d" % ti, [128, D], F32) for i in range(2)]
                        xh = [self.tile(s1, ("t0_xh%d" % i) + "_p%d" % ti, [128, D], BF16) for i in range(5)]
                        xl = [self.tile(s1, ("t0_xl%d" % i) + "_p%d" % ti, [128, D], BF16) for i in range(5)]
                        subs = [(x_loc[t0 + j * 128:t0 + (j + 1) * 128, :], 128, j * 128) for j in range(4)]
                        if len(segs) > 1:
                            subs.append((meta[:, :], NMETA, 512))
                        for j, (srcap, n, col) in enumerate(subs):
                            b_ = j % 2
                            k.dma("sp", xt[b_][0:n, :], srcap, writes=[xt[b_].r])
                            k.op("dve", lambda e: e.tensor_copy(out=xh[j][0:n, :], in_=xt[b_][0:n, :]),
                                 reads=[xt[b_].r], writes=[xh[j].r])
                            k.op("dve", lambda e: e.tensor_tensor(out=xl[j][0:n, :], in0=xt[b_][0:n, :],
                                                                  in1=xh[j][0:n, :], op=ALU.subtract),
                                 reads=[xt[b_].r, xh[j].r], writes=[xl[j].r])
                        for c in range(KD):
                            p = A[c % 2]
                            for j, (srcap, n, col) in enumerate(subs):
                                pp = p if j < 4 else Mb[c % 2]
                                oc = col if j < 4 else 0
                                for part, xx in enumerate((xh, xl)):
                                    k.op("pe", lambda e: e.matmul(pp[:, oc:oc + n], lhsT=xx[j][0:n, c * 128:(c + 1) * 128],
                                                                  rhs=identb[0:n, 0:n], start=(part == 0), stop=(part == 1)),
                                         reads=[xx[j].r, identb.r], writes=[pp.r])
                            k.op("act", lambda e: e.activation(out=hT[:, c, 0:512], in_=p[:, 0:512], func=AF.Copy),
                                 reads=[p.r], writes=[hT.r])
                            if len(segs) > 1:
                                k.op("act", lambda e: e.activation(out=hT[:, c, 512:528], in_=Mb[c % 2][:, 0:NMETA],
                                                                   func=AF.Copy),
                                     reads=[Mb[c % 2].r], writes=[hT.r])
                        k.barrier()
                else:
                    k.dma("sp", hT[:, :, 0:512], h1T[:, :, t0:t0 + 512], reads=[self.dres["h1T"]], writes=[hT.r])
                with contextlib.ExitStack() as s2:
                    acts = self.tile(s2, ("t%d_acts" % layer) + "_p%d" % ti, [128, KM, W], BF16)
                    wt = [self.tile(s2, "t%d_wm%d_p%d" % (layer, i, ti), [128, KM, 128], BF16) for i in range(3)]
                    def load_acts(lo, w, dlo):
                        srcs = acts_src(lo, w)
                        per = KM // len(srcs)
                        for i_, s_ap in enumerate(srcs):
                            k.dma("sp", acts[:, i_ * per:(i_ + 1) * per, dlo:dlo + w], s_ap, reads=[R_acts],
                                  writes=[acts.r])
                    load_acts(t0, 512, 0)
                    if len(segs) > 1:
                        load_acts(TPC, NMETA, 512)
                    for cb in range(KD):
                        for (p, lo, w) in gemm(wm, cb, KM, acts, segs, wt[cb % 3]):
                            k.op("dve", lambda e: e.tensor_tensor(out=hT[:, cb, lo:lo + w], in0=p[:, 0:w],
                                                                  in1=hT[:, cb, lo:lo + w], op=ALU.add),
                                 reads=[p.r, hT.r], writes=[hT.r])
                    k.barrier()
                with contextlib.ExitStack() as s3:
                    sq = self.tile(s3, ("t%d_sq" % layer) + "_p%d" % ti, [128, KD, W], BF16)
                    stats(segs, sq)
                    for c in range(KD):
                        for (lo, w) in segs:
                            k.op("dve", lambda e: e.scalar_tensor_tensor(
                                out=uT[:, c, lo:lo + w], in0=hT[:, c, lo:lo + w], scalar=self.c_gains[:, gi_ffn, c:c + 1],
                                in1=rstd[:, lo:lo + w], op0=ALU.mult, op1=ALU.mult),
                                 reads=[hT.r, rstd.r, self.c_gains.r], writes=[uT.r])
                    k.barrier()
                with contextlib.ExitStack() as s4:
                    actT = self.tile(s4, ("t%d_actT" % layer) + "_p%d" % ti, [128, KF, W], BF16)
                    wg_ = [self.tile(s4, "t%d_wg%d_p%d" % (layer, i, ti), [128, KD, 128], BF16) for i in range(3)]
                    wu_ = [self.tile(s4, "t%d_wu%d_p%d" % (layer, i, ti), [128, KD, 128], BF16) for i in range(3)]
                    wd_ = [self.tile(s4, "t%d_wd%d_p%d" % (layer, i, ti), [128, KF, 128], BF16) for i in range(3)]
                    sg = [self.tile(s4, "t%d_sg%d_p%d" % (layer, i, ti), [128, W], F32) for i in range(2)]
                    for j in range(KF):
                        gs = gemm(wgu, j, KD, uT, segs, wg_[j % 3])
                        us = gemm(wgu, KF + j, KD, uT, segs, wu_[j % 3], extra=[U0, UM])
                        for (pg, lo, w), (pu, _, _) in zip(gs, us):
                            k.op("act", lambda e: e.activation(out=sg[j % 2][:, lo:lo + w], in_=pg[:, 0:w], func=AF.Silu),
                                 reads=[pg.r], writes=[sg[j % 2].r])
                            k.op("dve", lambda e: e.tensor_tensor(out=actT[:, j, lo:lo + w], in0=pu[:, 0:w],
                                                                  in1=sg[j % 2][:, lo:lo + w], op=ALU.mult),
                                 reads=[pu.r, sg[j % 2].r], writes=[actT.r])
                    for cb in range(KD):
                        for (p, lo, w) in gemm(wdn, cb, KF, actT, segs, wd_[cb % 3]):
                            k.op("dve", lambda e: e.tensor_tensor(out=hT[:, cb, lo:lo + w], in0=p[:, 0:w],
                                                                  in1=hT[:, cb, lo:lo + w], op=ALU.add),
                                 reads=[p.r, hT.r], writes=[hT.r])
                    k.barrier()
                with contextlib.ExitStack() as s5:
                    sq = self.tile(s5, ("t%d_sq5" % layer) + "_p%d" % ti, [128, KD, W], BF16)
                    stats(segs, sq)
                    if layer == 0:
                        hn = self.tile(s5, ("t0_hn") + "_p%d" % ti, [128, KD, W], BF16)
                        for c in range(KD):
                            for (lo, w) in segs:
                                k.op("dve", lambda e: e.tensor_tensor(out=hn[:, c, lo:lo + w], in0=hT[:, c, lo:lo + w],
                                                                      in1=rstd[:, lo:lo + w], op=ALU.mult),
                                     reads=[hT.r, rstd.r], writes=[hn.r])
                        k.dma("sp", hn_loc[:, :, t0:t0 + 512], hn[:, :, 0:512], reads=[hn.r],
                              writes=[self.dres["hn_loc"]])
                        k.dma("sp", h1T[:, :, t0:t0 + 512], hT[:, :, 0:512], reads=[hT.r], writes=[self.dres["h1T"]])
                        if len(segs) > 1:
                            k.dma("sp", hn_meta, hn[:, :, 512:528], reads=[hn.r], writes=[self.dres["hn_meta"]])
                            k.dma("sp", h1T[:, :, TPC:TPC + NMETA], hT[:, :, 512:528], reads=[hT.r],
                                  writes=[self.dres["h1T"]])
                    else:
                        yT = self.tile(s5, ("t1_yT") + "_p%d" % ti, [128, KD, 512], F32)
                        yh = self.tile(s5, ("t1_yh") + "_p%d" % ti, [128, KD, 512], BF16)
                        yl = self.tile(s5, ("t1_yl") + "_p%d" % ti, [128, KD, 512], BF16)
                        osb = [self.tile(s5, ("t1_osb%d" % i) + "_p%d" % ti, [128, D], F32) for i in range(2)]
                        for c in range(KD):
                            k.op("dve", lambda e: e.scalar_tensor_tensor(
                                out=yT[:, c, :], in0=hT[:, c, 0:512], scalar=self.c_gains[:, 5, c:c + 1],
                                in1=rstd[:, 0:512], op0=ALU.mult, op1=ALU.mult),
                                 reads=[hT.r, rstd.r, self.c_gains.r], writes=[yT.r])
                        k.op("dve", lambda e: e.tensor_copy(out=yh[:], in_=yT[:]), reads=[yT.r], writes=[yh.r])
                        k.op("dve", lambda e: e.tensor_tensor(out=yl[:], in0=yT[:], in1=yh[:], op=ALU.subtract),
                             reads=[yT.r, yh.r], writes=[yl.r])
                        for j in range(4):
                            o_ = osb[j % 2]
                            for c4 in range(4):
                                p = A[(j * 4 + c4) % 2]
                                for ci in range(4):
                                    c = c4 * 4 + ci
                                    for part, yy in enumerate((yh, yl)):
                                        k.op("pe", lambda e: e.matmul(p[:, ci * 128:(ci + 1) * 128],
                                                                      lhsT=yy[:, c, j * 128:(j + 1) * 128],
                                                                      rhs=identb[:, :], start=(part == 0), stop=(part == 1)),
                                             reads=[yy.r, identb.r], writes=[p.r])
                                k.op("act", lambda e: e.activation(out=o_[:, c4 * 512:(c4 + 1) * 512], in_=p[:, :],
                                                                   func=AF.Copy),
                                     reads=[p.r], writes=[o_.r])
                            k.dma("sp", out[t0 + j * 128:t0 + (j + 1) * 128, :], o_[:], reads=[o_.r],
                                  writes=[self.dres["out"]])
                    k.barrier()
            k.barrier()
        if layer == 0 and self.fused:
            self.dint("hn_all", [NCORES * D, TPC], BF16)
            k.collective("AllGather", self.dram["hn_loc"].ap().opt(), self.dram["hn_all"].ap().opt(),
                         [list(range(NCORES))], reads=[self.dres["hn_loc"]], writes=[self.dres["hn_all"]])

    def phase2(self):
        self.token_phase(0)

    def phase4(self):
        self.token_phase(1)

    def phase3(self):
        k, nc, S, TPC = self.k, self.nc, self.S, self.TPC
        self.din("wq_c", [D, 256]); self.din("wk_c", [D, 256]); self.din("wv_c", [D, 256])
        self.din("nltri_bf", [128, 128], BF16); self.din("nones_bf", [128, 128], BF16)
        self.din("dmask", [128, 4, 512], BF16)
        hn_all = self.buf("hn_all", [NCORES * D, TPC], BF16, 2, 3).ap()
        hn_meta = self.buf("hn_meta", [D, NMETA], BF16, 2, 3).ap().rearrange("(c p) t -> p c t", p=128)
        o_send = self.buf("o_send", [NCORES, 256, TPC + NMETA], BF16, 3, 3).ap()
        NKT = S // 128 + 1
        with contextlib.ExitStack() as st:
            T_ = lambda n, s, d, **kw: self.tile(st, "p3_" + n, s, d, **kw)
            Wq = T_("Wq", [128, KD, 256], BF16); Wk = T_("Wk", [128, KD, 256], BF16); Wv = T_("Wv", [128, KD, 256], BF16)
            nltri = T_("nltri", [128, 128], BF16); nones = T_("nones", [128, 128], BF16)
            dmask = T_("dmask", [128, 4, 512], BF16)
            for tl, nm in ((nltri, "nltri_bf"), (nones, "nones_bf"), (dmask, "dmask")):
                k.dma("sp", tl[:], self.dram[nm].ap(), writes=[tl.r])
            with contextlib.ExitStack() as s0:
                wtmp = self.tile(s0, "p3_wtmp", [128, KD, 256], F32)
                for Wt_, nm, gi, sc in ((Wq, "wq_c", 2, 128 ** -0.5), (Wk, "wk_c", 4, 1.0), (Wv, "wv_c", 4, 1.0)):
                    k.dma("sp", wtmp[:], self.dram[nm].ap().rearrange("(c p) n -> p c n", p=128), writes=[wtmp.r])
                    k.op("dve", lambda e: e.scalar_tensor_tensor(
                        out=Wt_[:], in0=wtmp[:], scalar=sc,
                        in1=bc(self.c_gains[:, gi, :].unsqueeze(2), [128, KD, 256]), op0=ALU.mult, op1=ALU.mult),
                         reads=[wtmp.r, self.c_gains.r], writes=[Wt_.r])
                k.barrier()
            qT = T_("qT", [128, 2, S], BF16)
            kT = T_("kT", [128, 2, S + NMETA], BF16)
            vs = T_("vs", [128, NKT, 256], BF16)
            hb = [T_("hb%d" % i, [128, KD, 512], BF16) for i in range(2)]
            e_sb = [T_("e%d" % i, [128, 512], F32) for i in range(2)]
            spb = [T_("sp%d" % i, [128, 512], BF16) for i in range(2)]
            wtl = [T_("w%d" % i, [128, 512], F32) for i in range(2)]
            att = [T_("att%d" % i, [128, 512], BF16) for i in range(2)]
            carry = T_("carry", [128, 512], F32)
            osb = [T_("osb%d" % i, [128, 512], BF16) for i in range(2)]
            PA = [T_("PA%d" % i, [128, 512], F32, psum=True) for i in range(2)]
            PW = [T_("PW%d" % i, [128, 512], F32, psum=True) for i in range(2)]
            PC = [T_("PC%d" % i, [128, 512], F32, psum=True) for i in range(2)]
            PO = T_("PO", [128, 512], F32, psum=True)
            sci = 0
            pa = 0
            for b in range(2):
                scs = [("meta", 0, NMETA)] + [("real", t0, 512) for t0 in range(0, S, 512)]
                for kind, t0, T in scs:
                    h_ = hb[sci % 2]; sci += 1
                    if kind == "meta":
                        k.dma("sp", h_[:, :, 0:T], hn_meta, reads=[self.dres["hn_meta"]], writes=[h_.r])
                    else:
                        rank = 4 * b + t0 // TPC
                        c0 = t0 % TPC
                        k.dma("sp", h_[:, :, 0:T],
                              hn_all[rank * D:(rank + 1) * D, c0:c0 + T].rearrange("(c p) t -> p c t", p=128),
                              reads=[self.dres["hn_all"]], writes=[h_.r])
                    for hd in range(2):
                        for Wt_, dst, off in ((Wk, kT, (0 if kind == "meta" else NMETA + t0)),) + \
                                (((Wq, qT, t0),) if kind == "real" else ()):
                            p = PA[pa % 2]; pa += 1
                            for c in range(KD):
                                k.op("pe", lambda e: e.matmul(p[:, 0:T], lhsT=Wt_[:, c, hd * 128:(hd + 1) * 128],
                                                              rhs=h_[:, c, 0:T], start=(c == 0), stop=(c == KD - 1)),
                                     reads=[Wt_.r, h_.r], writes=[p.r])
                            k.op("act", lambda e: e.activation(out=dst[:, hd, off:off + T], in_=p[:, 0:T], func=AF.Copy),
                                 reads=[p.r], writes=[dst.r])
                    for q0 in range(0, T, 128):
                        Q = min(128, T - q0)
                        tix = 0 if kind == "meta" else 1 + (t0 + q0) // 128
                        p = PA[pa % 2]; pa += 1
                        for c in range(KD):
                            k.op("pe", lambda e: e.matmul(p[0:Q, 0:256], lhsT=h_[:, c, q0:q0 + Q], rhs=Wv[:, c, :],
                                                          start=(c == 0), stop=(c == KD - 1)),
                                 reads=[Wv.r, h_.r], writes=[p.r])
                        k.op("act", lambda e: e.activation(out=vs[0:Q, tix, :], in_=p[0:Q, 0:256], func=AF.Copy),
                             reads=[p.r], writes=[vs.r])
                it = 0
                for hd in range(2):
                    for qb in range(S // 512):
                        qs = slice(qb * 512, (qb + 1) * 512)
                        k.op("dve", lambda e: e.memset(carry[:], 0.0), writes=[carry.r])
                        tiles = [("diag", 4 * qb + r, r) for r in (3, 2, 1, 0)] + \
                                [("full", kt, 0) for kt in range(4 * qb - 1, -1, -1)] + [("meta", -1, 0)]
                        for ti, (kind, kt, r) in enumerate(tiles):
                            Kp = NMETA if kind == "meta" else 128
                            kc0 = 0 if kind == "meta" else NMETA + kt * 128
                            tix = 0 if kind == "meta" else 1 + kt
                            last = (ti == len(tiles) - 1)
                            i2 = it % 2; it += 1
                            pw, pc = PW[i2], PC[i2]
                            k.op("pe", lambda e: e.matmul(pw[0:Kp, :], lhsT=kT[:, hd, kc0:kc0 + Kp], rhs=qT[:, hd, qs],
                                                          start=True, stop=False),
                                 reads=[kT.r, qT.r], writes=[pw.r])
                            k.op("act", lambda e: e.activation(out=e_sb[i2][0:Kp, :], in_=pw[0:Kp, :], func=AF.Exp),
                                 reads=[pw.r], writes=[e_sb[i2].r])
                            k.op("act", lambda e: e.activation(out=spb[i2][0:Kp, :], in_=e_sb[i2][0:Kp, :], func=AF.Ln,
                                                               bias=self.c_one[0:Kp, 0:1]),
                                 reads=[e_sb[i2].r], writes=[spb[i2].r])
                            if kind == "diag":
                                k.op("pool", lambda e: e.tensor_tensor(out=spb[i2][:, :], in0=spb[i2][:, :],
                                                                       in1=dmask[:, r, :], op=ALU.mult),
                                     reads=[spb[i2].r, dmask.r], writes=[spb[i2].r])
                            k.op("pe", lambda e: e.matmul(pw[0:Kp, :], lhsT=nltri[0:Kp, 0:Kp], rhs=spb[i2][0:Kp, :],
                                                          start=False, stop=True),
                                 reads=[nltri.r, spb[i2].r], writes=[pw.r])
                            if not last:
                                k.op("pe", lambda e: e.matmul(pc[:, :], lhsT=nones[0:Kp, :], rhs=spb[i2][0:Kp, :],
                                                              start=True, stop=True),
                                     reads=[nones.r, spb[i2].r], writes=[pc.r])
                            k.op("dve", lambda e: e.tensor_tensor(out=wtl[i2][0:Kp, :], in0=pw[0:Kp, :],
                                                                  in1=carry[0:Kp, :], op=ALU.add),
                                 reads=[pw.r, carry.r], writes=[wtl[i2].r])
                            if not last:
                                k.op("dve", lambda e: e.tensor_tensor(out=carry[:, :], in0=pc[:, :], in1=carry[:, :],
                                                                      op=ALU.add),
                                     reads=[pc.r, carry.r], writes=[carry.r])
                            k.op("act", lambda e: e.activation(out=att[i2][0:Kp, :], in_=wtl[i2][0:Kp, :], func=AF.Exp),
                                 reads=[wtl[i2].r], writes=[att[i2].r])
                            if kind == "diag":
                                k.op("pool", lambda e: e.tensor_tensor(out=att[i2][:, :], in0=att[i2][:, :],
                                                                       in1=dmask[:, r, :], op=ALU.mult),
                                     reads=[att[i2].r, dmask.r], writes=[att[i2].r])
                            k.op("pe", lambda e: e.matmul(PO[:, :], lhsT=vs[0:Kp, tix, hd * 128:(hd + 1) * 128],
                                                          rhs=att[i2][0:Kp, :], start=(ti == 0), stop=last),
                                 reads=[vs.r, att[i2].r], writes=[PO.r])
                        ob = osb[(hd * (S // 512) + qb) % 2]
                        k.op("act", lambda e: e.activation(out=ob[:, :], in_=PO[:, :], func=AF.Copy),
                             reads=[PO.r], writes=[ob.r])
                        rank = 4 * b + (qb * 512) // TPC
                        c0 = (qb * 512) % TPC
                        k.dma("sp", o_send[rank, hd * 128:(hd + 1) * 128, c0:c0 + 512], ob[:, :], reads=[ob.r],
                              writes=[self.dres["o_send"]])
            k.barrier()
        if self.fused:
            self.dint("o_all", [NCORES * NCORES, 256, TPC + NMETA], BF16)
            k.collective("AllGather", self.dram["o_send"].ap().opt(), self.dram["o_all"].ap().opt(),
                         [list(range(NCORES))], reads=[self.dres["o_send"]], writes=[self.dres["o_all"]])

    def build(self):
        k = self.k
        with contextlib.ExitStack() as st:
            self.load_consts(st)
            self.c_eps = self.tile(st, "c_eps", [128, 1], F32)
            k.op("dve", lambda e: e.memset(self.c_eps[:], EPS), writes=[self.c_eps.r])
            self.c_one = self.tile(st, "c_one", [128, 1], F32)
            k.op("dve", lambda e: e.memset(self.c_one[:], 1.0), writes=[self.c_one.r])
            k.barrier()
            for ph in self.phases:
                getattr(self, "phase%d" % ph)()
            outs = [self.dres[n] for n in self.ext_out]
            for name in self.dumps:
                t = self.dram[name]
                o = self.dout("dump_" + name, list(t.shape), t.dtype)
                k.dma("sp", o.ap(), t.ap(), reads=[self.dres[name]], writes=[self.dres["dump_" + name]])
                outs.append(self.dres["dump_" + name])
            k.final_wait(outs)
            k.barrier()
        return self.nc


def _bf(a):
    return np.asarray(a, dtype=np.float32).astype(ml_dtypes.bfloat16)


def _WFULL(inp):
    f = lambda a: np.asarray(a, np.float32)
    return {"op": f(inp["ssd_out_proj"][0]), "gu0": f(inp["ffn_gate_up"][0]), "dn0": f(inp["ffn_down"][0]),
            "wo": f(inp["sb_w_o"][0]), "gu1": f(inp["ffn_gate_up"][1]), "dn1": f(inp["ffn_down"][1])}


def make_in_maps(inp, S):
    TPC = S // 4
    x = np.asarray(inp["x"], np.float32)
    gains = np.stack([inp["norm_mix"][0], inp["norm_ffn"][0], inp["norm_mix"][1], inp["norm_ffn"][1],
                      inp["kv_norm"], inp["final_norm"]], 0).astype(np.float32)
    gains = np.ascontiguousarray(gains.reshape(6, KD, 128).transpose(2, 0, 1))
    maps = []
    for c in range(NCORES):
        b, s = c // 4, c % 4
        m = {
            "x_loc": np.ascontiguousarray(x[b, s * TPC:(s + 1) * TPC]),
            "meta": np.asarray(inp["meta_tokens"], np.float32),
            "gains": gains,
            "ident_bf": _bf(np.eye(128)),
            "ident_f32": np.eye(128, dtype=np.float32),
        }
        g = c
        wi = np.asarray(inp["ssd_in_proj"][0], np.float32)
        zc = wi[:, g * 512:(g + 1) * 512]
        xc = wi[:, DI + g * 512:DI + (g + 1) * 512]
        Bc = wi[:, 2 * DI + g * 128:2 * DI + (g + 1) * 128]
        Cc = wi[:, 2 * DI + 1024 + g * 128:2 * DI + 1024 + (g + 1) * 128]
        dc = wi[:, 2 * DI + 2048 + g * 8:2 * DI + 2048 + (g + 1) * 8]
        m["w_in_g"] = np.ascontiguousarray(np.concatenate([zc, xc, Bc, Cc, dc], 1))
        cwf = np.asarray(inp["ssd_conv_w"][0], np.float32)
        cbf = np.asarray(inp["ssd_conv_b"][0], np.float32)
        chans = np.concatenate([np.arange(g * 512, (g + 1) * 512), DI + g * 128 + np.arange(128),
                                DI + 1024 + g * 128 + np.arange(128)])
        m["conv_w"] = np.ascontiguousarray(cwf[:, chans].reshape(4, 6, 128).transpose(2, 1, 0))
        m["conv_b"] = np.ascontiguousarray(cbf[chans].reshape(6, 128).T)
        m["dt_bias"] = np.asarray(inp["ssd_dt_bias"][0], np.float32)[g * 8:(g + 1) * 8].reshape(1, 8)
        m["ones_bf"] = _bf(np.ones((128, 128)))
        hs = slice(c * 256, (c + 1) * 256)
        m["wq_c"] = np.ascontiguousarray(np.asarray(inp["sb_w_q"][0], np.float32)[:, hs])
        m["wk_c"] = np.ascontiguousarray(np.asarray(inp["w_kv"], np.float32)[:, hs])
        m["wv_c"] = np.ascontiguousarray(np.asarray(inp["w_kv"], np.float32)[:, D + c * 256:D + (c + 1) * 256])
        m["nltri_bf"] = _bf(-np.tril(np.ones((128, 128), np.float32)))
        m["nones_bf"] = _bf(-np.ones((128, 128)))
        jj = np.arange(128)[:, None, None]; rr = np.arange(4)[None, :, None]; tt = np.arange(512)[None, None, :]
        m["dmask"] = _bf((tt > jj + 128 * rr).astype(np.float32))
        for name, wfull in _WFULL(inp).items():
            CB, KC = MK.WSPEC[name]
            cbs = CB // NCORES
            cols = wfull[:, c * cbs * 128:(c + 1) * cbs * 128]
            til = cols.reshape(KC, 128, cbs, 128).transpose(2, 1, 0, 3)
            m["ws_" + name] = np.ascontiguousarray(til).reshape(-1, 2048)
        m["alog"] = np.asarray(inp["ssd_a_log"][0], np.float32)[g * 8:(g + 1) * 8].reshape(1, 8)
        m["dvec"] = np.asarray(inp["ssd_d"][0], np.float32)[g * 8:(g + 1) * 8].reshape(1, 8)
        m["ssd_norm"] = np.asarray(inp["ssd_norm"][0], np.float32)[g * 512:(g + 1) * 512].reshape(1, 512)
        tri = np.triu(np.ones((128, 128), np.float32))
        m["tri_bf"] = _bf(tri)
        m["sl_bf"] = _bf(1.0 - tri)
        m["tri_f32"] = tri
        m["ones_f32"] = np.ones((128, 128), np.float32)
        maps.append(m)
    return maps


def host_reshard(ph, outs, state):
    for c in range(NCORES):
        for n, v in outs[c].items():
            state[c][n] = np.asarray(v)
    if ph == 0:
        allg = np.concatenate([np.asarray(outs[c]["uT_loc"]) for c in range(NCORES)], 0)
        for c in range(NCORES):
            state[c]["uT_all"] = allg
    if ph == 1:
        for c in range(NCORES):
            state[c]["g_mine"] = np.ascontiguousarray(
                np.stack([np.asarray(outs[s]["g_send"])[c] for s in range(NCORES)], 0))
    if ph == 2:
        allg = np.concatenate([np.asarray(outs[c]["hn_loc"]) for c in range(NCORES)], 0)
        for c in range(NCORES):
            state[c]["hn_all"] = allg
    if ph == 3:
        for c in range(NCORES):
            state[c]["o_mine"] = np.ascontiguousarray(
                np.stack([np.asarray(outs[s]["o_send"])[c] for s in range(NCORES)], 0))
    if ph == 5:
        for name in MK.WSPEC:
            full = np.concatenate([np.asarray(outs[c]["wsb_" + name]) for c in range(NCORES)], 0)
            for c in range(NCORES):
                state[c]["Wt_" + name] = full


FUSED = True


def _run(nc, maps):
    return run_bass_kernel_spmd(nc, maps, core_ids=list(range(NCORES))).results


def kernel(**inputs):
    inp = {k_: np.asarray(v) for k_, v in inputs.items()}
    S = inp["x"].shape[1]
    TPC = S // 4
    base = make_in_maps(inp, S)
    if FUSED:
        mk = MK(S, phases=(0, 1, 5, 2, 3, 4), fused=True)
        nc = mk.build()
        maps = [{n: base[c][n] for n in mk.ext_in} for c in range(NCORES)]
        outs = _run(nc, maps)
    else:
        state = [{} for _ in range(NCORES)]
        for ph in (5, 0, 1, 2, 3, 4):
            mk = MK(S, phases=[ph], fused=False)
            nc = mk.build()
            maps = [{n: (state[c][n] if n in state[c] else base[c][n]) for n in mk.ext_in} for c in range(NCORES)]
            outs = _run(nc, maps)
            host_reshard(ph, outs, state)
            for c in range(NCORES):
                for n in list(state[c]):
                    if ph == 2 and n in ("Wt_op", "Wt_gu0", "Wt_dn0", "g_mine", "g_send", "uT_all", "uT_loc"):
                        del state[c][n]
                    elif ph == 5 and n.startswith("wsb_"):
                        del state[c][n]
    out = np.empty((2, S, D), np.float32)
    for c in range(NCORES):
        out[c // 4, (c % 4) * TPC:(c % 4 + 1) * TPC] = np.asarray(outs[c]["out"], np.float32)
    return out
```
